# Optimizing a Trainium2 kernel written in Bass

```python
import math
import jax, jax.numpy as jnp
from jax import lax
import numpy as np

D_MODEL = 1024
BATCH = 8
SEQ = 2048
DEPTH = 4

GRID_W = 64
CTX_LEN = 256
N_MIXERS = 4
EPS = 1e-6
ROPE_THETA = 10000.0
Q_BLOCK = 128

FNET_GROUPS = 8
FNET_GROUP_DIM = D_MODEL // FNET_GROUPS
DIFF_HEAD_DIM = 64
DIFF_HEADS = D_MODEL // (2 * DIFF_HEAD_DIM)
HGRN_EXPAND = 128
HGRN_HEADS = D_MODEL // HGRN_EXPAND
HGRN_HEAD_V = D_MODEL // HGRN_HEADS
HGRN_FORGET_DIM = HGRN_HEADS * HGRN_EXPAND
HGRN_CHUNK = 64
GQA_HEAD_DIM = 128
GQA_Q_HEADS = D_MODEL // GQA_HEAD_DIM
GQA_KV_HEADS = 2
GQA_GROUP = GQA_Q_HEADS // GQA_KV_HEADS
D_FF = -(-8 * D_MODEL // (3 * 256)) * 256
N_FNET_LAYERS = len(range(0, DEPTH, N_MIXERS))
N_DIFF_LAYERS = len(range(1, DEPTH, N_MIXERS))
N_HGRN_LAYERS = len(range(2, DEPTH, N_MIXERS))
N_GQA_LAYERS = len(range(3, DEPTH, N_MIXERS))

kernel_name = 'hybrid_interleaved_dit_trunk'


def rms_norm(x, gain):
    xf = x.astype(jnp.float32)
    y = xf * lax.rsqrt(jnp.mean(xf * xf, axis=-1, keepdims=True) + EPS)
    return (y * gain.astype(jnp.float32)).astype(x.dtype)


def modulate(h, shift, scale):
    return h * (1.0 + scale) + shift


def swiglu(h, w_in, w_out):
    gate, up = jnp.split(h @ w_in, 2, axis=-1)
    return (jax.nn.silu(gate) * up) @ w_out


def axial_rope_tables(rows, head_dim):
    row = jnp.repeat(jnp.arange(rows, dtype=jnp.float32), GRID_W)
    col = jnp.tile(jnp.arange(GRID_W, dtype=jnp.float32), rows)
    n_freq = head_dim // 4
    inv_freq = ROPE_THETA ** (-jnp.arange(n_freq, dtype=jnp.float32) / n_freq)
    ang = jnp.concatenate([row[:, None] * inv_freq, col[:, None] * inv_freq], axis=-1)
    return jnp.cos(ang), jnp.sin(ang)


def apply_rope(x, cos, sin):
    shape = (cos.shape[0],) + (1,) * (x.ndim - 3) + (cos.shape[1],)
    cos = cos.reshape(shape)
    sin = sin.reshape(shape)
    x1, x2 = jnp.split(x.astype(jnp.float32), 2, axis=-1)
    return jnp.concatenate([x1 * cos - x2 * sin, x1 * sin + x2 * cos], axis=-1).astype(x.dtype)


def sweep_query_blocks(q, attend):
    b, l = q.shape[:2]
    nb = l // Q_BLOCK
    qb = jnp.moveaxis(q.reshape((b, nb, Q_BLOCK) + q.shape[2:]), 1, 0)
    out = lax.map(attend, qb)
    return jnp.moveaxis(out, 0, 1).reshape((b, l) + out.shape[3:])


def fnet_mixer(h_lat, h_ctx, w_out, b_out, need_ctx):
    def mix(h):
        b, l, _ = h.shape
        hg = h.astype(jnp.float32).reshape(b, l, FNET_GROUPS, FNET_GROUP_DIM)
        y = jnp.fft.fftn(hg, axes=(1, 3), norm='ortho').real
        return y.reshape(b, l, D_MODEL).astype(h.dtype) @ w_out + b_out
    return mix(h_lat), (mix(h_ctx) if need_ctx else None)


def diff_attn_core(q, k, v, lam):
    s = jnp.einsum('bqhcd,bkhcd->bhcqk', q.astype(jnp.float32), k.astype(jnp.float32)) * DIFF_HEAD_DIM ** -0.5
    p = jax.nn.softmax(s, axis=-1)
    w = p[:, :, 0] - lam * p[:, :, 1]
    return jnp.einsum('bhqk,bkhe->bqhe', w, v.astype(jnp.float32)).astype(v.dtype)


def diff_attention_mixer(h_lat, h_ctx, w_in, q_gain, k_gain, lam_par, subln_gain, w_out, layer_idx, rows, need_ctx):
    lam_init = 0.8 - 0.6 * math.exp(-0.3 * layer_idx)
    lp = lam_par.astype(jnp.float32)
    lam = jnp.exp(jnp.sum(lp[0] * lp[1])) - jnp.exp(jnp.sum(lp[2] * lp[3])) + lam_init
    cos, sin = axial_rope_tables(rows, DIFF_HEAD_DIM)

    def project(h, with_q):
        b, l, _ = h.shape
        if with_q:
            q, k, v = jnp.split(h @ w_in, 3, axis=-1)
            q = rms_norm(q.reshape(b, l, DIFF_HEADS, 2, DIFF_HEAD_DIM), q_gain)
        else:
            k, v = jnp.split(h @ w_in[:, D_MODEL:], 2, axis=-1)
            q = None
        k = rms_norm(k.reshape(b, l, DIFF_HEADS, 2, DIFF_HEAD_DIM), k_gain)
        return q, k, v.reshape(b, l, DIFF_HEADS, 2 * DIFF_HEAD_DIM)

    def finish(o):
        b, l = o.shape[:2]
        o = rms_norm(o, subln_gain) * (1.0 - lam_init)
        return o.reshape(b, l, D_MODEL) @ w_out

    q_c, k_c, v_c = project(h_ctx, need_ctx)
    q_l, k_l, v_l = project(h_lat, True)
    q_l = apply_rope(q_l, cos, sin)
    k_l = apply_rope(k_l, cos, sin)
    k_all = jnp.concatenate([k_c, k_l], axis=1)
    v_all = jnp.concatenate([v_c, v_l], axis=1)
    o_lat = finish(sweep_query_blocks(q_l, lambda qb: diff_attn_core(qb, k_all, v_all, lam)))
    o_ctx = finish(diff_attn_core(q_c, k_c, v_c, lam)) if need_ctx else None
    return o_lat, o_ctx


def gla_chunk_scan(q, k, v, log_f, s0):
    b, l, h, dk = q.shape
    dv = v.shape[-1]
    n = l // HGRN_CHUNK

    def chunks(t):
        return jnp.moveaxis(t.reshape((b, n, HGRN_CHUNK) + t.shape[2:]), 1, 0)

    lower = jnp.tril(jnp.ones((HGRN_CHUNK, HGRN_CHUNK), dtype=bool))[None, :, :, None, None]

    def step(s, inp):
        qc, kc, vc, gc = inp
        bc = jnp.cumsum(gc, axis=1)
        decay = jnp.exp(jnp.where(lower, bc[:, :, None] - bc[:, None, :], -jnp.inf))
        scores = jnp.einsum('bthk,bshk,btshk->bhts', qc, kc, decay)
        o = jnp.einsum('bhts,bshv->bthv', scores, vc) + jnp.einsum('bthk,bhkv->bthv', qc * jnp.exp(bc), s)
        b_last = bc[:, -1]
        s = jnp.exp(b_last)[..., None] * s + jnp.einsum('bshk,bshv->bhkv', kc * jnp.exp(b_last[:, None] - bc), vc)
        return s, o

    s_fin, o = lax.scan(step, s0, (chunks(q), chunks(k), chunks(v), chunks(log_f)))
    return jnp.moveaxis(o, 0, 1).reshape(b, l, h, dv), s_fin


def hgrn2_mixer(h_lat, h_ctx, w_in, lower_bound, norm_gain, w_out, layer_idx, need_ctx):
    lbs = jnp.cumsum(jax.nn.softmax(lower_bound.astype(jnp.float32), axis=1), axis=1)
    lb = (lbs[:, layer_idx] - lbs[:, 0]).reshape(2, HGRN_HEADS, HGRN_EXPAND)

    def project(h):
        b, l, _ = h.shape
        q, z_fwd, z_bwd, v, g = jnp.split(h @ w_in, 5, axis=-1)
        q = jax.nn.silu(q).reshape(b, l, HGRN_HEADS, HGRN_EXPAND).astype(jnp.float32)
        v = v.reshape(b, l, HGRN_HEADS, HGRN_HEAD_V).astype(jnp.float32)
        log_f = [jnp.log(lb[d] + (1.0 - lb[d]) * jax.nn.sigmoid(z.reshape(b, l, HGRN_HEADS, HGRN_EXPAND).astype(jnp.float32)))
                 for d, z in enumerate((z_fwd, z_bwd))]
        return q, v, log_f, g

    def run(q, v, lf, s0, reverse):
        if reverse:
            q, v, lf = jnp.flip(q, 1), jnp.flip(v, 1), jnp.flip(lf, 1)
        o, s = gla_chunk_scan(q, -jnp.expm1(lf), v, lf, s0)
        return (jnp.flip(o, 1) if reverse else o), s

    def finish(o, g):
        b, l = o.shape[:2]
        o = rms_norm(o, norm_gain) * jax.nn.silu(g.reshape(b, l, HGRN_HEADS, HGRN_HEAD_V).astype(jnp.float32))
        return o.reshape(b, l, D_MODEL).astype(g.dtype) @ w_out

    q_c, v_c, lf_c, g_c = project(h_ctx)
    q_l, v_l, lf_l, g_l = project(h_lat)
    s0 = jnp.zeros((h_ctx.shape[0], HGRN_HEADS, HGRN_EXPAND, HGRN_HEAD_V), jnp.float32)
    o_ctx_f, s_ctx_f = run(q_c, v_c, lf_c[0], s0, False)
    o_ctx_b, s_ctx_b = run(q_c, v_c, lf_c[1], s0, True)
    o_lat_f, _ = run(q_l, v_l, lf_l[0], s_ctx_f, False)
    o_lat_b, _ = run(q_l, v_l, lf_l[1], s_ctx_b, True)
    o_lat = finish(o_lat_f + o_lat_b, g_l)
    o_ctx = finish(o_ctx_f + o_ctx_b, g_c) if need_ctx else None
    return o_lat, o_ctx


def gqa_core(q, k, v):
    s = jnp.einsum('bqhgd,bkhd->bhgqk', q.astype(jnp.float32), k.astype(jnp.float32)) * GQA_HEAD_DIM ** -0.5
    p = jax.nn.softmax(s, axis=-1)
    return jnp.einsum('bhgqk,bkhd->bqhgd', p, v.astype(jnp.float32)).astype(v.dtype)


def gqa_mixer(h_lat, h_ctx, w_in, q_gain, k_gain, w_out, rows, need_ctx):
    qd = GQA_Q_HEADS * GQA_HEAD_DIM
    cos, sin = axial_rope_tables(rows, GQA_HEAD_DIM)

    def project(h, with_q):
        b, l, _ = h.shape
        if with_q:
            q, kv = jnp.split(h @ w_in, [qd], axis=-1)
            q = rms_norm(q.reshape(b, l, GQA_KV_HEADS, GQA_GROUP, GQA_HEAD_DIM), q_gain)
        else:
            kv = h @ w_in[:, qd:]
            q = None
        k, v = jnp.split(kv, 2, axis=-1)
        k = rms_norm(k.reshape(b, l, GQA_KV_HEADS, GQA_HEAD_DIM), k_gain)
        return q, k, v.reshape(b, l, GQA_KV_HEADS, GQA_HEAD_DIM)

    def finish(o):
        b, l = o.shape[:2]
        return o.reshape(b, l, D_MODEL) @ w_out

    q_c, k_c, v_c = project(h_ctx, need_ctx)
    q_l, k_l, v_l = project(h_lat, True)
    q_l = apply_rope(q_l, cos, sin)
    k_l = apply_rope(k_l, cos, sin)
    k_all = jnp.concatenate([k_c, k_l], axis=1)
    v_all = jnp.concatenate([v_c, v_l], axis=1)
    o_lat = finish(sweep_query_blocks(q_l, lambda qb: gqa_core(qb, k_all, v_all)))
    o_ctx = finish(gqa_core(q_c, k_c, v_c)) if need_ctx else None
    return o_lat, o_ctx


def setup_inputs(seed: int = 0) -> dict:
    key = jax.random.key(seed)
    ks = iter(jax.random.split(key, 32))
    D = D_MODEL

    def nrm(shape, scale):
        return jax.random.normal(next(ks), shape, jnp.float32) * scale

    def gain(shape):
        return 1.0 + nrm(shape, 0.05)

    return {
        'x': nrm((BATCH, SEQ, D), 1.0),
        'c': nrm((BATCH, D), 1.0),
        'ctx': nrm((BATCH, CTX_LEN, D), 1.0),
        'c_ctx': nrm((D,), 1.0),
        'w_ada': nrm((DEPTH, D, 6 * D), 0.5 * D ** -0.5),
        'b_ada': nrm((DEPTH, 6 * D), 0.02),
        'norm_gain': gain((DEPTH, 2, D)),
        'ffn_w_in': nrm((DEPTH, D, 2 * D_FF), D ** -0.5),
        'ffn_w_out': nrm((DEPTH, D_FF, D), D_FF ** -0.5),
        'fnet_w_out': nrm((N_FNET_LAYERS, D, D), D ** -0.5),
        'fnet_b_out': nrm((N_FNET_LAYERS, D), 0.02),
        'diff_w_in': nrm((N_DIFF_LAYERS, D, 3 * D), D ** -0.5),
        'diff_q_gain': gain((N_DIFF_LAYERS, 2, DIFF_HEAD_DIM)),
        'diff_k_gain': gain((N_DIFF_LAYERS, 2, DIFF_HEAD_DIM)),
        'diff_lambda': nrm((N_DIFF_LAYERS, 4, DIFF_HEAD_DIM), 0.1),
        'diff_subln_gain': gain((N_DIFF_LAYERS, 2 * DIFF_HEAD_DIM)),
        'diff_w_out': nrm((N_DIFF_LAYERS, D, D), D ** -0.5),
        'hgrn_w_in': nrm((N_HGRN_LAYERS, D, 5 * D), D ** -0.5),
        'hgrn_lower_bound': nrm((2, DEPTH, HGRN_FORGET_DIM), 1.0),
        'hgrn_norm_gain': gain((N_HGRN_LAYERS, HGRN_HEAD_V)),
        'hgrn_w_out': nrm((N_HGRN_LAYERS, D, D), D ** -0.5),
        'gqa_w_in': nrm((N_GQA_LAYERS, D, (GQA_Q_HEADS + 2 * GQA_KV_HEADS) * GQA_HEAD_DIM), D ** -0.5),
        'gqa_q_gain': gain((N_GQA_LAYERS, GQA_HEAD_DIM)),
        'gqa_k_gain': gain((N_GQA_LAYERS, GQA_HEAD_DIM)),
        'gqa_w_out': nrm((N_GQA_LAYERS, D, D), D ** -0.5),
    }


def reference(x, c, ctx, c_ctx, w_ada, b_ada, norm_gain, ffn_w_in, ffn_w_out, fnet_w_out, fnet_b_out,
              diff_w_in, diff_q_gain, diff_k_gain, diff_lambda, diff_subln_gain, diff_w_out,
              hgrn_w_in, hgrn_lower_bound, hgrn_norm_gain, hgrn_w_out,
              gqa_w_in, gqa_q_gain, gqa_k_gain, gqa_w_out):
    rows = x.shape[1] // GRID_W
    cond_lat = jax.nn.silu(c)[:, None, :]
    cond_ctx = jax.nn.silu(c_ctx)[None, None, :]
    for i in range(DEPTH):
        m, j = i % N_MIXERS, i // N_MIXERS
        need_ctx = i < DEPTH - 1
        sh1, sc1, g1, sh2, sc2, g2 = jnp.split(cond_lat @ w_ada[i] + b_ada[i], 6, axis=-1)
        csh1, csc1, cg1, csh2, csc2, cg2 = jnp.split(cond_ctx @ w_ada[i] + b_ada[i], 6, axis=-1)
        h_lat = modulate(rms_norm(x, norm_gain[i, 0]), sh1, sc1)
        h_ctx = modulate(rms_norm(ctx, norm_gain[i, 0]), csh1, csc1) if (need_ctx or m != 0) else None
        if m == 0:
            o_lat, o_ctx = fnet_mixer(h_lat, h_ctx, fnet_w_out[j], fnet_b_out[j], need_ctx)
        elif m == 1:
            o_lat, o_ctx = diff_attention_mixer(h_lat, h_ctx, diff_w_in[j], diff_q_gain[j], diff_k_gain[j], diff_lambda[j],
                                                diff_subln_gain[j], diff_w_out[j], i, rows, need_ctx)
        elif m == 2:
            o_lat, o_ctx = hgrn2_mixer(h_lat, h_ctx, hgrn_w_in[j], hgrn_lower_bound, hgrn_norm_gain[j], hgrn_w_out[j], i, need_ctx)
        else:
            o_lat, o_ctx = gqa_mixer(h_lat, h_ctx, gqa_w_in[j], gqa_q_gain[j], gqa_k_gain[j], gqa_w_out[j], rows, need_ctx)
        x = x + g1 * o_lat
        x = x + g2 * swiglu(modulate(rms_norm(x, norm_gain[i, 1]), sh2, sc2), ffn_w_in[i], ffn_w_out[i])
        if need_ctx:
            ctx = ctx + cg1 * o_ctx
            ctx = ctx + cg2 * swiglu(modulate(rms_norm(ctx, norm_gain[i, 1]), csh2, csc2), ffn_w_in[i], ffn_w_out[i])
    return x
```

```python
import math
from contextlib import ExitStack

import numpy as np
import ml_dtypes

import concourse.bass as bass
import concourse.mybir as mybir
from concourse.bass_utils import run_bass_kernel_spmd

F32 = mybir.dt.float32
BF16 = mybir.dt.bfloat16
AF = mybir.ActivationFunctionType
ALU = mybir.AluOpType
AX = mybir.AxisListType

D = 1024
KC = 8
SEQ = 2048
CTX = 256
T = SEQ + CTX
NT = T // 128
DEPTH = 4
DFF = 2816
NJ = DFF // 128
EPS = 1e-6
BLKS = [(0, 256), (256, 512), (768, 512), (1280, 512), (1792, 512)]
THIRDS = [(0, 8), (8, 16), (16, 22)]

SAME_ENGINE_SYNC = True


class Sched:
    def __init__(self, nc, stack, n_dma_sems=12):
        self.nc = nc
        self.engs = {"pe": nc.tensor, "act": nc.scalar, "dve": nc.vector, "pool": nc.gpsimd, "sp": nc.sync}
        self.sem = {}
        self.cnt = {}
        for e in ("pe", "act", "dve", "pool"):
            self.sem[e] = stack.enter_context(nc.semaphore("s_" + e))
            self.cnt[e] = 0
        self.dsem = {}
        self.dcnt = {}
        self.drr = {}
        for q in ("sp", "pool"):
            self.dsem[q] = [stack.enter_context(nc.semaphore(f"d_{q}{i}")) for i in range(n_dma_sems)]
            self.dcnt[q] = [0] * n_dma_sems
            self.drr[q] = 0
        self.semobj = {}
        for e, s in self.sem.items():
            self.semobj[("c", e)] = s
        for q, lst in self.dsem.items():
            for i, s in enumerate(lst):
                self.semobj[("d", q, i)] = s
        self.waited = {e: {} for e in self.engs}
        self.alias = {}
        self.res = {}
        self.n_inst = 0
        self.n_wait = 0

    def _need(self, eng, deps):
        best = {}
        for ev in deps:
            if ev is None:
                continue
            sk, val, src = ev
            if src == eng and sk[0] == "c" and (eng == "pe" or not SAME_ENGINE_SYNC):
                continue
            if best.get(sk, 0) < val:
                best[sk] = val
        e = self.engs[eng]
        for sk, val in best.items():
            if self.waited[eng].get(sk, 0) >= val:
                continue
            e.wait_ge(self.semobj[sk], val)
            self.n_wait += 1
            self.waited[eng][sk] = val

    def _x(self, keys):
        out = []
        for k in keys:
            out.extend(self.alias.get(k, (k,)))
        return out

    def dma_split(self, q, out, in_, key, reads=()):
        n = out.shape[1]
        h = n // 2
        subs = ((key, "lo"), (key, "hi"))
        self.alias[key] = subs
        self.dma(q, out[:, 0:h], in_[:, 0:h], reads=reads, writes=[subs[0]])
        self.dma(q, out[:, h:n], in_[:, h:n], reads=reads, writes=[subs[1]])

    def _deps(self, reads, writes):
        reads, writes = self._x(reads), self._x(writes)
        deps = []
        for k in reads:
            st = self.res.get(k)
            if st is not None:
                deps.append(st[0])
        for k in writes:
            st = self.res.get(k)
            if st is not None:
                deps.append(st[0])
                deps.extend(st[1])
        return deps

    def _commit(self, ev, reads, writes):
        reads, writes = self._x(reads), self._x(writes)
        for k in reads:
            st = self.res.setdefault(k, [None, []])
            st[1].append(ev)
        for k in writes:
            self.res[k] = [ev, []]

    def op(self, eng, fn, reads=(), writes=()):
        self._need(eng, self._deps(reads, writes))
        ins = fn()
        self.cnt[eng] += 1
        ins.then_inc(self.sem[eng], 1)
        ev = (("c", eng), self.cnt[eng], eng)
        self._commit(ev, reads, writes)
        self.n_inst += 1
        return ev

    def dma(self, q, out, in_, reads=(), writes=(), **kw):
        i = self.drr[q]
        self.drr[q] = (i + 1) % len(self.dsem[q])
        sk = ("d", q, i)
        deps = self._deps(reads, writes)
        if self.dcnt[q][i] > 0:
            deps.append((sk, self.dcnt[q][i], "dma"))
        self._need(q, deps)
        ins = self.engs[q].dma_start(out=out, in_=in_, **kw)
        self.dcnt[q][i] += 16
        ins.then_inc(self.dsem[q][i], 16)
        ev = (sk, self.dcnt[q][i], "dma")
        self._commit(ev, reads, writes)
        self.n_inst += 1
        return ev

    def barrier(self):
        evs = [(("c", e), self.cnt[e], "x") for e in self.cnt if self.cnt[e] > 0]
        for q in self.dsem:
            for i, v in enumerate(self.dcnt[q]):
                if v > 0:
                    evs.append((("d", q, i), v, "dma"))
        for eng in self.engs:
            self._need(eng, evs)

    def wait_all(self, eng, keys):
        deps = []
        for k in self._x(keys):
            st = self.res.get(k)
            if st is not None:
                deps.append(st[0])
                deps.extend(st[1])
        self._need(eng, deps)


def _bf(a):
    return np.ascontiguousarray(a.astype(ml_dtypes.bfloat16))


_CONST_CACHE = {}


def make_consts():
    if _CONST_CACHE:
        return _CONST_CACHE
    c = {}
    c["ident_f"] = np.eye(128, dtype=np.float32)
    c["ident_b"] = _bf(np.eye(128, dtype=np.float32))
    k = np.arange(128)
    m = (k[:, None] * k[None, :]) % 128
    ang = 2 * np.pi * m / 128.0
    c["cs128"] = _bf(np.concatenate([np.cos(ang), -np.sin(ang)], axis=1) / math.sqrt(128.0))

    def dft(L):
        l = np.arange(L, dtype=np.int64)
        mm = (l[:, None] * l[None, :]) % L
        a = 2 * np.pi * mm / float(L)
        C = np.cos(a) / math.sqrt(L)
        S_ = np.sin(a) / math.sqrt(L)
        nl = L // 128
        out = np.empty((nl, 128, 2, nl, 128), dtype=np.float32)
        for z, M in enumerate((C, S_)):
            out[:, :, z] = M.reshape(nl, 128, nl, 128).transpose(2, 1, 0, 3)
        return _bf(out)

    c["dft_lat"] = dft(SEQ)
    c["dft_ctx"] = dft(CTX)

    def rope(hd):
        half = hd // 2
        nf = hd // 4
        t = np.arange(SEQ)
        row = (t // 64).astype(np.float32)
        colv = (t % 64).astype(np.float32)
        inv = (np.float32(10000.0) ** (-np.arange(nf, dtype=np.float32) / np.float32(nf))).astype(np.float32)
        ang = np.concatenate([row[:, None] * inv[None, :], colv[:, None] * inv[None, :]], axis=1).astype(np.float32)
        cosv = np.cos(ang.astype(np.float64)).astype(np.float32)
        sinv = np.sin(ang.astype(np.float64)).astype(np.float32)
        out = np.empty((128, 2, SEQ), dtype=np.float32)
        for p in range(128):
            i = p % hd
            out[p, 0] = cosv[:, i % half]
            out[p, 1] = (-1.0 if i < half else 1.0) * sinv[:, i % half]
        return np.ascontiguousarray(out)

    def rotidx(hd):
        p = np.arange(128)
        i = p % hd
        half = hd // 2
        return (p - i) + (i + half) % hd

    c["rope64"] = rope(64)
    c["rope128"] = rope(128)
    for hd in (64, 128):
        r = rotidx(hd)
        pm = np.zeros((128, 128), dtype=np.float32)
        pm[r, np.arange(128)] = 1.0
        c[f"perm{hd}"] = pm
        bo = np.zeros((128, 128), dtype=np.float32)
        for m_ in range(128):
            g0 = (m_ // hd) * hd
            bo[g0:g0 + hd, m_] = 1.0 / hd
        c[f"bo{hd}"] = _bf(bo)
    tt = np.arange(64)
    c["hmask"] = np.ascontiguousarray(np.stack([(tt[:, None] <= tt[None, :]), (tt[:, None] >= tt[None, :])], axis=1).astype(np.float32))
    cm = np.ones((128, T), dtype=np.float32)
    cm[:, 0::64] = 0.0
    c["cmscan"] = _bf(cm)
    msk = np.zeros((128, 2), dtype=np.float32)
    msk[:64, 0] = 1.0
    msk[64:, 1] = 1.0
    c["cmask"] = msk
    _CONST_CACHE.update(c)
    return c


class Builder:
    def __init__(self, n_layers=DEPTH, debug=False):
        self.n_layers = n_layers
        self.debug = debug

    def sb(self, name, shape, dt):
        return self.st.enter_context(self.nc.sbuf_tensor(name, shape, dt))

    def build(self):
        nc = bass.Bass("TRN2", target_bir_lowering=False)
        self.nc = nc
        dram = lambda n, s, d, k="ExternalInput": nc.dram_tensor(n, list(s), d, kind=k).ap()
        I = {}
        I["x"] = dram("x", [SEQ, D], F32)
        I["c"] = dram("c", [KC, 128], F32)
        I["ctx"] = dram("ctx", [CTX, D], F32)
        I["c_ctx"] = dram("c_ctx", [KC, 128], F32)
        I["w_ada"] = dram("w_ada", [DEPTH, D, 6 * D], F32)
        I["b_ada"] = dram("b_ada", [DEPTH * 48, 128], F32)
        I["norm_gain"] = dram("norm_gain", [DEPTH * 2 * KC, 128], F32)
        I["ffn_w_in"] = dram("ffn_w_in", [DEPTH, D, 2 * DFF], F32)
        I["ffn_w_out"] = dram("ffn_w_out", [DEPTH, DFF, D], F32)
        I["fnet_w_out"] = dram("fnet_w_out", [D, D], F32)
        I["fnet_b_out"] = dram("fnet_b_out", [KC, 128], F32)
        I["diff_w_in"] = dram("diff_w_in", [D, 3 * D], F32)
        I["diff_w_out"] = dram("diff_w_out", [D, D], F32)
        I["diff_g"] = dram("diff_g", [4, 128], F32)
        I["diff_lambda"] = dram("diff_lambda", [256], F32)
        I["diff_subln"] = dram("diff_subln", [128], F32)
        I["gqa_w_in"] = dram("gqa_w_in", [D, 1536], F32)
        I["gqa_w_out"] = dram("gqa_w_out", [D, D], F32)
        I["gqa_g"] = dram("gqa_g", [4, 128], F32)
        I["hgrn_w_in"] = dram("hgrn_w_in", [D, 5 * D], F32)
        I["hgrn_w_out"] = dram("hgrn_w_out", [D, D], F32)
        I["hgrn_lb"] = dram("hgrn_lb", [64, 128], F32)
        I["hgrn_gain"] = dram("hgrn_gain", [128], F32)
        I["hmask"] = dram("hmask", [64, 2, 64], F32)
        I["cmscan"] = dram("cmscan", [128, T], BF16)
        I["rope64"] = dram("rope64", [128, 2, SEQ], F32)
        I["rope128"] = dram("rope128", [128, 2, SEQ], F32)
        I["perm64"] = dram("perm64", [128, 128], F32)
        I["perm128"] = dram("perm128", [128, 128], F32)
        I["bo64"] = dram("bo64", [128, 128], BF16)
        I["bo128"] = dram("bo128", [128, 128], BF16)
        I["cmask"] = dram("cmask", [128, 2], F32)
        I["ident_f"] = dram("ident_f", [128, 128], F32)
        I["ident_b"] = dram("ident_b", [128, 128], BF16)
        I["cs128"] = dram("cs128", [128, 256], BF16)
        I["dft_lat"] = dram("dft_lat", [16, 128, 2, 16, 128], BF16)
        I["dft_ctx"] = dram("dft_ctx", [2, 128, 2, 2, 128], BF16)
        self.I = I
        self.out = dram("out", [SEQ, D], F32, "ExternalOutput")
        if self.debug:
            self.dbg = dram("dbg", [T, D], F32, "ExternalOutput")

        with ExitStack() as st:
            self.st = st
            S = Sched(nc, st)
            self.S = S
            self.XT = self.sb("XT", [128, KC, T], F32)
            self.ARENA = self.sb("ARENA", [128, 36864], BF16)
            self.HT = self.ARENA[:, 0:18432].rearrange("p (c t) -> p c t", c=KC)
            self.BIGOFF = 18432
            self.SCR = self.sb("SCR", [128, 23552], BF16)
            self.identf = self.sb("identf", [128, 128], F32)
            self.identb = self.sb("identb", [128, 128], BF16)
            self.onesb = self.sb("onesb", [128, 128], BF16)
            self.epsT = self.sb("epsT", [128, 1], F32)
            self.condT = self.sb("condT", [128, KC, 2], F32)
            self.NG = self.sb("NG", [128, DEPTH * 2 * KC], F32)
            self.BADA = self.sb("BADA", [128, DEPTH * 48], F32)
            self.MODS = [self.sb("MODa", [128, 6, KC, 2], F32), self.sb("MODb", [128, 6, KC, 2], F32)]
            self.condTb = self.sb("condTb", [128, KC, 2], BF16)
            self.STGS = self.sb("STGS", [128, 1, 128], F32)
            self.SQ = self.sb("SQ", [128, KC, 512], BF16)
            self.RS = self.sb("RS", [128, 1, 512], F32)
            self.TMPF = self.sb("TMPF", [128, 2, 512], F32)
            P01 = st.enter_context(nc.psum_tensor("P01", [128, 1024], F32))
            P23 = st.enter_context(nc.psum_tensor("P23", [128, 1024], F32))
            P45 = st.enter_context(nc.psum_tensor("P45", [128, 1024], F32))
            P6 = st.enter_context(nc.psum_tensor("P6", [128, 512], F32))
            self.P = [P01[:, 0:512], P01[:, 512:1024], P23[:, 0:512], P23[:, 512:1024], P45[:, 0:512], P45[:, 512:1024], P6[:, :]]
            self.PACC = [P23[:, :], P45[:, :]]
            self.acc_unit = 0
            self.ada_next = 0
            self.norm_done = False
            self.outkeys = []
            self.out_done = set()
            self.woh_cnt = 0
            self.PTB = st.enter_context(nc.psum_tensor("PTB", [128, 1024], BF16))
            self.tmpf_i = 0
            self.stg_i = 0

            self.emit_init()
            S.barrier()
            for i in range(self.n_layers):
                self.MOD = self.MODS[i % 2]
                self.modkey = ("MOD", i % 2)
                if not self.norm_done:
                    self.adaln_finish(i, 1)
                m = i % 4
                if m == 0:
                    self.emit_fnet(i)
                elif m == 1:
                    self.emit_attn(i, "diff")
                elif m == 3:
                    self.emit_attn(i, "gqa")
                else:
                    self.emit_hgrn(i)
                S.barrier()
                self.emit_ffn(i)
                S.barrier()
            self.emit_output()
            print("instructions", S.n_inst, "waits", S.n_wait)
        return nc

    def scr(self, off_elems, n_elems, dt=BF16):
        v = self.SCR[:, off_elems:off_elems + n_elems]
        if dt == F32:
            v = v.bitcast(F32)
        return v

    def big(self, off_elems, n_elems, dt=BF16):
        v = self.ARENA[:, self.BIGOFF + off_elems:self.BIGOFF + off_elems + n_elems]
        if dt == F32:
            v = v.bitcast(F32)
        return v

    def load_fm(self, dst, src, R, dst_key):
        nc, S = self.nc, self.S
        si = 0
        stg = self.STGS[0:R, si, 0:128]
        S.dma("sp", stg, src, writes=[("STGS", si)])
        pt = self.P[6]
        S.op("pe", lambda: nc.tensor.transpose(pt[:, 0:R], stg, self.identf[0:R, 0:R]),
             reads=[("STGS", si), "identf"], writes=[("P", 6)])
        S.op("dve", lambda: nc.vector.tensor_copy(out=dst, in_=pt[:, 0:R]), reads=[("P", 6)], writes=[dst_key])

    def emit_init(self):
        nc, S, I = self.nc, self.S, self.I
        S.dma("sp", self.identf[:], I["ident_f"], writes=["identf"])
        S.dma("sp", self.identb[:], I["ident_b"], writes=["identb"])
        S.op("dve", lambda: nc.vector.memset(self.onesb[:], 1.0 / D), writes=["onesb"])
        S.op("dve", lambda: nc.vector.memset(self.epsT[:], EPS), writes=["epsT"])
        craw = self.TMPF[:, 0, 0:16]
        self.load_fm(craw[:, 0:8], I["c"], 8, "craw0")
        self.load_fm(craw[:, 8:16], I["c_ctx"], 8, "craw1")
        S.op("act", lambda: nc.scalar.activation(out=self.condT[:, :, 0], in_=craw[:, 0:8], func=AF.Silu),
             reads=["craw0"], writes=["condT0"])
        S.op("act", lambda: nc.scalar.activation(out=self.condT[:, :, 1], in_=craw[:, 8:16], func=AF.Silu),
             reads=["craw1"], writes=["condT1"])
        S.op("dve", lambda: nc.vector.tensor_copy(out=self.condTb[:], in_=self.condT[:]), reads=["condT0", "condT1"], writes=["condTb"])
        self.load_fm(self.NG[:, 0:64], I["norm_gain"], 64, "NG")
        self.load_fm(self.BADA[:, 0:128], I["b_ada"][0:128, :], 128, "BADA0")
        self.load_fm(self.BADA[:, 128:192], I["b_ada"][128:192, :], 64, "BADA1")
        XIO = self.scr(0, 4096, F32).rearrange("p (a n) -> p a n", a=2)
        for tg in range(NT):
            src = I["ctx"][tg * 128:(tg + 1) * 128, :] if tg < 2 else I["x"][(tg - 2) * 128:(tg - 1) * 128, :]
            si = tg % 2
            S.dma("sp", XIO[:, si, :], src, writes=[("XIO", si)])
            blk = self.blk_of_tile(tg)
            for h in range(2):
                pb = self.P[h + 2 * (tg % 2)]
                pk = ("P", h + 2 * (tg % 2))
                for cc in range(4):
                    c = h * 4 + cc
                    S.op("pe", lambda: nc.tensor.transpose(pb[:, cc * 128:(cc + 1) * 128],
                                                           XIO[:, si, c * 128:(c + 1) * 128], self.identf[:]),
                         reads=[("XIO", si), "identf"], writes=[pk])
                eng = "dve" if h == 0 else "act"
                dst = self.XT[:, h * 4:(h + 1) * 4, tg * 128:(tg + 1) * 128]
                srcp = pb[:].rearrange("p (c t) -> p c t", c=4)
                wk = [("XT", h * 4 + cc, blk) for cc in range(4)]
                if eng == "dve":
                    S.op("dve", lambda: nc.vector.tensor_copy(out=dst, in_=srcp), reads=[pk], writes=wk)
                else:
                    S.op("act", lambda: nc.scalar.activation(out=dst, in_=srcp, func=AF.Copy), reads=[pk], writes=wk)
            for _ in range(2 if tg < 6 else 1):
                self.adaln_block(0, self.ada_next)
                self.ada_next += 1

    @staticmethod
    def blk_of_tile(tg):
        return 0 if tg < 2 else 1 + (tg - 2) // 4

    def adaln_block(self, i, og):
        nc, S, I = self.nc, self.S, self.I
        p6 = self.P[6]
        WA = self.scr(18432, 4096).rearrange("p (b k n) -> p b k n", b=2, k=KC)
        ROW2 = self.scr(22528, 1024, F32).rearrange("p (b n) -> p b n", b=2)
        MODN = self.MODS[i % 2]
        mk = ("MOD", i % 2, og // 4)
        self.adaln_flush()
        b = og % 2
        ROW = ROW2[0:2, b, :]
        S.dma_split("pool", WA[:, b], I["w_ada"][i, :, og * 256:(og + 1) * 256].rearrange("(k p) n -> p k n", p=128), ("WA", b))
        for kc in range(KC):
            S.op("pe", lambda: nc.tensor.matmul(p6[0:2, 0:256], self.condTb[:, kc, :], WA[:, b, kc, :], start=(kc == 0), stop=(kc == KC - 1)),
                 reads=[("WA", b), "condTb"], writes=[("P", 6)])
        S.op("act", lambda: nc.scalar.activation(out=ROW, in_=p6[0:2, 0:256], func=AF.Copy), reads=[("P", 6)], writes=[("ROW", b)])

        def part_b():
            for o2 in range(2):
                S.op("pe", lambda: nc.tensor.transpose(p6[:, 256 + o2 * 2:258 + o2 * 2], ROW[:, o2 * 128:(o2 + 1) * 128], self.identf[0:2, 0:2]),
                     reads=[("ROW", b), "identf"], writes=[("P", 6)])
            modv = MODN[:].rearrange("p s c l -> p (s c) l")[:, og * 2:og * 2 + 2, :]
            bad = self.BADA[:, i * 48 + og * 2:i * 48 + og * 2 + 2].unsqueeze(2).to_broadcast([128, 2, 2])
            S.op("dve", lambda: nc.vector.tensor_tensor(out=modv, in0=p6[:, 256:260].rearrange("p (o l) -> p o l", l=2), in1=bad, op=ALU.add),
                 reads=[("P", 6), "BADA0", "BADA1"], writes=[mk])

        self.adaln_pending = part_b

    def adaln_flush(self):
        if getattr(self, "adaln_pending", None) is not None:
            f = self.adaln_pending
            self.adaln_pending = None
            f()

    def adaln_finish(self, i, which):
        nc, S = self.nc, self.S
        self.adaln_flush()
        MODN = self.MODS[i % 2]
        s, gsel = (1, 0) if which == 1 else (4, 1)
        mk = ("MOD", i % 2, s)
        g = self.NG[:, (i * 2 + gsel) * KC:(i * 2 + gsel + 1) * KC].unsqueeze(2).to_broadcast([128, KC, 2])
        S.op("dve", lambda: nc.vector.scalar_tensor_tensor(out=MODN[:, s], in0=MODN[:, s], scalar=1.0, in1=g,
                                                           op0=ALU.add, op1=ALU.mult),
             reads=[mk, "NG"], writes=[mk])

    def norm_mod_all(self, blks, which, dst, after_block=None):
        for _ in self.norm_mod_gen(blks, which, dst, after_block):
            pass

    def norm_mod_gen(self, blks, which, dst, after_block=None, MOD=None, modkey=None):
        nc, S = self.nc, self.S
        sA, sB = (1, 0) if which == 1 else (4, 3)
        pm = self.P[6]
        MOD = self.MOD if MOD is None else MOD
        modkey = self.modkey if modkey is None else modkey

        def stage_a(blk):
            t0, n = BLKS[blk]
            S.op("act", lambda: nc.scalar.activation(out=self.SQ[:, 0:4, 0:n], in_=self.XT[:, 0:4, t0:t0 + n], func=AF.Square),
                 reads=[("XT", c, blk) for c in range(4)], writes=[("SQ", 0)])
            S.op("dve", lambda: nc.vector.tensor_tensor(out=self.SQ[:, 4:8, 0:n], in0=self.XT[:, 4:8, t0:t0 + n], in1=self.XT[:, 4:8, t0:t0 + n], op=ALU.mult),
                 reads=[("XT", c, blk) for c in range(4, 8)], writes=[("SQ", 1)])
            for c in range(KC):
                S.op("pe", lambda: nc.tensor.matmul(pm[:, 0:n], self.onesb[:], self.SQ[:, c, 0:n], start=(c == 0), stop=(c == KC - 1)),
                     reads=[("SQ", c // 4), "onesb"], writes=[("P", 6)])

        def stage_b(blk):
            t0, n = BLKS[blk]
            rs = self.RS[:, 0, 0:n]
            S.op("act", lambda: nc.scalar.activation(out=rs, in_=pm[:, 0:n], func=AF.Ln, bias=self.epsT[:, 0:1]),
                 reads=[("P", 6), "epsT"], writes=["RS"])
            S.op("act", lambda: nc.scalar.activation(out=rs, in_=rs, func=AF.Exp, scale=-0.5), reads=["RS"], writes=["RS"])

        def stage_c(blk):
            t0, n = BLKS[blk]
            col = 1 if blk == 0 else 0
            rs = self.RS[:, 0, 0:n]
            for c in range(KC):
                ti = self.tmpf_i
                self.tmpf_i ^= 1
                tmp = self.TMPF[:, ti, 0:n]
                S.op("dve", lambda: nc.vector.tensor_tensor(out=tmp, in0=self.XT[:, c, t0:t0 + n], in1=rs, op=ALU.mult),
                     reads=[("XT", c, blk), "RS"], writes=[("TMPF", ti)])
                d_ap, d_key = dst(blk, c)
                S.op("act", lambda: nc.scalar.activation(out=d_ap, in_=tmp, func=AF.Identity,
                                                          scale=MOD[:, sA, c, col:col + 1], bias=MOD[:, sB, c, col:col + 1]),
                     reads=[("TMPF", ti), modkey + (sA,), modkey + (sB,)], writes=[d_key])
            if after_block is not None:
                after_block(blk)

        prev = None
        for blk in blks:
            stage_a(blk)
            if prev is not None:
                stage_c(prev)
            stage_b(blk)
            prev = blk
            yield
        stage_c(prev)

    def emit_fnet(self, i):
        nc, S, I = self.nc, self.S, self.I
        HB = self.scr(0, 4096).rearrange("p (c t) -> p c t", c=KC)
        CS = self.scr(4096, 256)
        WFO = self.scr(4352, 8192).rearrange("p (k n) -> p k n", k=KC)
        DF = self.scr(12544, 2 * 4096).rearrange("p (b z l q) -> p b z l q", b=2, z=2, l=16)
        YTOK = self.scr(20736, 1024)
        FBO = self.scr(21760, 16, F32)
        ZT = self.ARENA[:, :].rearrange("p (t z n) -> p t z n", t=NT, z=2)
        YT = HB

        S.dma("sp", CS, I["cs128"], writes=["CS"])
        S.dma_split("pool", WFO, I["fnet_w_out"].rearrange("(k p) n -> p k n", p=128), "WFO")
        self.load_fm(FBO, I["fnet_b_out"], 8, "FBO")

        def z_block(blk):
            t0, n = BLKS[blk]
            for tl in range(n // 128):
                tg = (t0 // 128) + tl
                for gp in range(4):
                    pz = self.P[gp % 2]
                    pk = ("P", gp % 2)
                    for gg in range(2):
                        g = gp * 2 + gg
                        S.op("pe", lambda: nc.tensor.matmul(pz[:, gg * 256:(gg + 1) * 256], HB[:, g, tl * 128:(tl + 1) * 128], CS,
                                                            start=True, stop=True),
                             reads=[("HB", g), "CS"], writes=[pk])
                    src = pz[:].rearrange("p (g z c) -> p z g c", g=2, z=2)
                    dstz = ZT[:, tg, :, gp * 256:(gp + 1) * 256].rearrange("p z (g c) -> p z g c", g=2)
                    if gp % 2 == 0:
                        S.op("dve", lambda: nc.vector.tensor_copy(out=dstz, in_=src), reads=[pk], writes=[("ZT", tg)])
                    else:
                        S.op("act", lambda: nc.scalar.activation(out=dstz, in_=src, func=AF.Copy), reads=[pk], writes=[("ZT", tg)])

        self.norm_mod_all(list(range(5)), 1, lambda blk, c: (HB[:, c, 0:BLKS[blk][1]], ("HB", c)), after_block=z_block)

        dcount = 0
        for blk in range(5):
            t0, n = BLKS[blk]
            col = 1 if blk == 0 else 0
            if blk == 0:
                nlc, tbase, dsrc = 2, 0, I["dft_ctx"]
            else:
                nlc, tbase, dsrc = 16, 2, I["dft_lat"]
            for tl in range(n // 128):
                j = (t0 // 128 - tbase) + tl
                b = dcount % 2
                dcount += 1
                S.dma("sp", DF[:, b, :, 0:nlc, :], dsrc[j], writes=[("DF", b)])
                for half in range(2):
                    py = self.P[2 + half]
                    pk = ("P", 2 + half)
                    nmm = 2 * nlc
                    k = 0
                    for z in range(2):
                        for lc in range(nlc):
                            S.op("pe", lambda: nc.tensor.matmul(py[:], DF[:, b, z, lc, :], ZT[:, tbase + lc, z, half * 512:(half + 1) * 512],
                                                                start=(k == 0), stop=(k == nmm - 1)),
                                 reads=[("DF", b), ("ZT", tbase + lc)], writes=[pk])
                            k += 1
                    if half == 0:
                        S.op("dve", lambda: nc.vector.tensor_copy(out=YTOK[:, 0:512], in_=py[:]), reads=[pk], writes=[("YTOK", 0)])
                    else:
                        S.op("act", lambda: nc.scalar.activation(out=YTOK[:, 512:1024], in_=py[:], func=AF.Copy), reads=[pk],
                             writes=[("YTOK", 1)])
                ptb = self.PTB
                for c in range(KC):
                    S.op("pe", lambda: nc.tensor.transpose(ptb[:, c * 128:(c + 1) * 128], YTOK[:, c * 128:(c + 1) * 128], self.identb[:]),
                         reads=[("YTOK", c // 4), "identb"], writes=[("P", 7)])
                S.op("dve", lambda: nc.vector.tensor_copy(out=YT[:, :, tl * 128:(tl + 1) * 128],
                                                          in_=ptb[:].rearrange("p (c t) -> p c t", c=KC)),
                     reads=[("P", 7)], writes=[("HB", c) for c in range(KC)])
            for oc in range(KC):
                po = self.P[5]
                for kc in range(KC):
                    S.op("pe", lambda: nc.tensor.matmul(po[:, 0:n], WFO[:, kc, oc * 128:(oc + 1) * 128], YT[:, kc, 0:n],
                                                        start=(kc == 0), stop=(kc == KC - 1)),
                         reads=["WFO", ("HB", kc)], writes=[("P", 5)])
                ti = self.tmpf_i
                self.tmpf_i ^= 1
                tmp = self.TMPF[:, ti, 0:n]
                S.op("act", lambda: nc.scalar.activation(out=tmp, in_=po[:, 0:n], func=AF.Identity, bias=FBO[:, oc:oc + 1]),
                     reads=[("P", 5), "FBO"], writes=[("TMPF", ti)])
                xs = self.XT[:, oc, t0:t0 + n]
                S.op("dve", lambda: nc.vector.scalar_tensor_tensor(out=xs, in0=tmp, scalar=self.MOD[:, 2, oc, col:col + 1], in1=xs,
                                                                   op0=ALU.mult, op1=ALU.add),
                     reads=[("TMPF", ti), self.modkey + (2,), ("XT", oc, blk)], writes=[("XT", oc, blk)])

    def emit_attn(self, i, kind):
        nc, S, I = self.nc, self.S, self.I
        need_ctx = i < DEPTH - 1
        diff = kind == "diff"
        hd = 64 if diff else 128
        scale = float(hd) ** -0.5
        w_in = I["diff_w_in"] if diff else I["gqa_w_in"]
        w_out = I["diff_w_out"] if diff else I["gqa_w_out"]
        ngroups = 8 if diff else 2
        nstreams = 2 if diff else 4
        nqw = 1 if diff else 4
        off = [0]

        def take(n):
            a = off[0]
            off[0] += n
            return a

        ROPE = self.scr(take(8192), 8192, F32).rearrange("p (z t) -> p z t", z=2)
        WQ = self.scr(take(4096), 4096).rearrange("p (k n) -> p k n", k=KC)
        WK = self.scr(take(1024), 1024).rearrange("p (k n) -> p k n", k=KC)
        WV = self.scr(take(1024), 1024).rearrange("p (k n) -> p k n", k=KC)
        WOH2 = self.scr(take(2048), 2048).rearrange("p (b n) -> p b n", b=2)
        QF = self.scr(take(1024), 1024, F32)
        RSQ = self.scr(take(1024), 1024, F32)
        T1 = self.scr(take(1024), 1024, F32)
        T2 = self.scr(take(1024), 1024, F32)
        PT = self.scr(take(1024), 1024).rearrange("p (b n) -> p b n", b=2)
        O1N = self.scr(take(1024), 1024, F32).rearrange("p (q e) -> p q e", q=4)
        WW = QF
        OB = self.SCR[:, 0:0]
        MISC = self.scr(take(64), 64, F32)
        SGB = self.scr(take(256), 256, F32)
        BO = self.scr(take(128), 128)
        PERM = self.scr(take(256), 256, F32)
        assert off[0] <= 23552
        OB = RSQ.bitcast(BF16)[:, 0:512]
        QT = [self.big(s_ * T, T) for s_ in range(nstreams)]
        KT = self.big(4 * T, T)
        V = self.big(5 * T, NT * 132).rearrange("p (t e) -> p t e", t=NT)
        OT = self.big(5 * T + NT * 132, T)
        mul, add = ALU.mult, ALU.add

        S.dma("sp", ROPE, I["rope64" if diff else "rope128"], writes=["ROPE"])
        S.dma("sp", BO, I["bo64" if diff else "bo128"], writes=["BO"])
        S.dma("sp", PERM, I["perm64" if diff else "perm128"], writes=["PERM"])
        self.load_fm(MISC[:, 0:4], I["diff_g" if diff else "gqa_g"], 4, "GAINS")
        S.dma("sp", MISC[:, 4:6], I["cmask"], writes=["CMASK"])
        S.op("pool", lambda: nc.gpsimd.memset(V[:, :, 128:129], 1.0), writes=["Vones"])
        S.op("pool", lambda: nc.gpsimd.memset(MISC[:, 24:25], -20.0), writes=["NEGC"])
        if diff:
            lam_init = 0.8 - 0.6 * math.exp(-0.3 * i)
            LPB = T1[:, 0:256]
            S.dma("sp", LPB, I["diff_lambda"].partition_broadcast(128), writes=["T1"])
            S.dma("sp", SGB, I["diff_subln"].partition_broadcast(128), writes=["SGB"])
            tmp = T2[:, 0:128]
            S.op("dve", lambda: nc.vector.tensor_tensor(out=tmp[:, 0:64], in0=LPB[:, 0:64], in1=LPB[:, 64:128], op=mul),
                 reads=["T1"], writes=["T2"])
            S.op("dve", lambda: nc.vector.tensor_tensor(out=tmp[:, 64:128], in0=LPB[:, 128:192], in1=LPB[:, 192:256], op=mul),
                 reads=["T1", "T2"], writes=["T2"])
            S.op("dve", lambda: nc.vector.reduce_sum(out=MISC[:, 11:12], in_=tmp[:, 0:64], axis=AX.X), reads=["T2"], writes=["LAM"])
            S.op("dve", lambda: nc.vector.reduce_sum(out=MISC[:, 12:13], in_=tmp[:, 64:128], axis=AX.X), reads=["T2", "LAM"], writes=["LAM"])
            S.op("act", lambda: nc.scalar.activation(out=MISC[:, 11:13], in_=MISC[:, 11:13], func=AF.Exp), reads=["LAM"], writes=["LAM"])
            S.op("dve", lambda: nc.vector.scalar_tensor_tensor(out=MISC[:, 6:7], in0=MISC[:, 12:13], scalar=-lam_init, in1=MISC[:, 11:12],
                                                               op0=add, op1=ALU.subtract), reads=["LAM"], writes=["NLAM"])
            S.op("dve", lambda: nc.vector.tensor_scalar_mul(out=SGB, in0=SGB, scalar1=1.0 - lam_init), reads=["SGB"], writes=["SGB"])

        if not self.norm_done:
            self.norm_mod_all(list(range(5)), 1, lambda blk, c: (self.HT[:, c, BLKS[blk][0]:BLKS[blk][0] + BLKS[blk][1]], ("HT", c, blk)))
        self.norm_done = False

        p4, p5, p6 = self.P[4], self.P[5], self.P[6]
        S.barrier()
        SQF = self.SQ[:].rearrange("p c n -> p (c n)")
        BSETS = [
            dict(proj=self.P[4], projk=("P", 4), pss=self.P[5], pssk=("P", 5), prm=self.P[6], prmk=("P", 6),
                 QF=QF, QFk="QF", sq=SQF[:, 0:512], sqk="SQa", RSQ=RSQ, RSQk="RSQ", T1=T1, T1k="T1", T2=T2, T2k="T2"),
            dict(proj=self.P[0], projk=("P", 0), pss=self.P[1], pssk=("P", 1), prm=self.P[2], prmk=("P", 2),
                 QF=SQF[:, 1024:2048].bitcast(F32), QFk="QF1", sq=SQF[:, 512:1024], sqk="SQb",
                 RSQ=SQF[:, 2048:3072].bitcast(F32), RSQk="RSQ1", T1=SQF[:, 3072:4096].bitcast(F32), T1k="T1b",
                 T2=self.TMPF[:, 0, :], T2k="T2b"),
        ]

        def qk_chain(kind, blk, sq_, st):
            B = BSETS[st]
            t0, n = BLKS[blk]
            gcol = 2 if kind == "k" else 0
            Wm = WK if kind == "k" else WQ[:, :, sq_ * 128:(sq_ + 1) * 128]
            wkey = "WK" if kind == "k" else "WQ"
            for kc in range(KC):
                S.op("pe", lambda: nc.tensor.matmul(B["proj"][:, 0:n], Wm[:, kc, :], self.HT[:, kc, t0:t0 + n], start=(kc == 0), stop=(kc == KC - 1)),
                     reads=[wkey, ("HT", kc, blk)], writes=[B["projk"]])
            yield
            S.op("act", lambda: nc.scalar.activation(out=B["QF"][:, 0:n], in_=B["proj"][:, 0:n], func=AF.Copy), reads=[B["projk"]], writes=[B["QFk"]])
            S.op("act", lambda: nc.scalar.activation(out=B["sq"][:, 0:n], in_=B["proj"][:, 0:n], func=AF.Square), reads=[B["projk"]], writes=[B["sqk"]])
            yield
            S.op("pe", lambda: nc.tensor.matmul(B["pss"][:, 0:n], BO, B["sq"][:, 0:n], start=True, stop=True), reads=["BO", B["sqk"]], writes=[B["pssk"]])
            if blk != 0:
                S.op("pe", lambda: nc.tensor.matmul(B["prm"][:, 0:n], PERM, B["QF"][:, 0:n], start=True, stop=True), reads=["PERM", B["QFk"]],
                     writes=[B["prmk"]])
            yield
            S.op("act", lambda: nc.scalar.activation(out=B["RSQ"][:, 0:n], in_=B["pss"][:, 0:n], func=AF.Ln, bias=self.epsT[:, 0:1]),
                 reads=[B["pssk"], "epsT"], writes=[B["RSQk"]])
            S.op("act", lambda: nc.scalar.activation(out=B["RSQ"][:, 0:n], in_=B["RSQ"][:, 0:n], func=AF.Exp, scale=-0.5), reads=[B["RSQk"]],
                 writes=[B["RSQk"]])
            yield
            t1 = B["T1"][:, 0:n]
            if blk == 0:
                S.op("dve", lambda: nc.vector.scalar_tensor_tensor(out=t1, in0=B["QF"][:, 0:n], scalar=MISC[:, gcol:gcol + 1], in1=B["RSQ"][:, 0:n],
                                                                   op0=mul, op1=mul), reads=[B["QFk"], B["RSQk"], "GAINS"], writes=[B["T1k"]])
            else:
                l0 = t0 - CTX
                t2 = B["T2"][:, 0:n]
                S.op("dve", lambda: nc.vector.scalar_tensor_tensor(out=t1, in0=B["QF"][:, 0:n], scalar=MISC[:, gcol:gcol + 1],
                                                                   in1=ROPE[:, 0, l0:l0 + n], op0=mul, op1=mul),
                     reads=[B["QFk"], "GAINS", "ROPE"], writes=[B["T1k"]])
                S.op("dve", lambda: nc.vector.scalar_tensor_tensor(out=t2, in0=B["prm"][:, 0:n], scalar=MISC[:, gcol + 1:gcol + 2],
                                                                   in1=ROPE[:, 1, l0:l0 + n], op0=mul, op1=mul),
                     reads=[B["prmk"], "GAINS", "ROPE"], writes=[B["T2k"]])
                yield
                S.op("dve", lambda: nc.vector.tensor_tensor(out=t1, in0=t1, in1=t2, op=add), reads=[B["T1k"], B["T2k"]], writes=[B["T1k"]])
                S.op("dve", lambda: nc.vector.tensor_tensor(out=t1, in0=t1, in1=B["RSQ"][:, 0:n], op=mul), reads=[B["T1k"], B["RSQk"]], writes=[B["T1k"]])
            yield
            if kind == "k":
                S.op("act", lambda: nc.scalar.activation(out=KT[:, t0:t0 + n], in_=t1, func=AF.Copy), reads=[B["T1k"]], writes=[("KT", blk)])
            elif diff:
                S.op("dve", lambda: nc.vector.tensor_scalar_mul(out=QT[0][:, t0:t0 + n], in0=t1, scalar1=MISC[:, 4:5]),
                     reads=[B["T1k"], "CMASK"], writes=[("QT", 0, blk)])
                S.op("act", lambda: nc.scalar.activation(out=QT[1][:, t0:t0 + n], in_=t1, func=AF.Copy, scale=MISC[:, 5:6]),
                     reads=[B["T1k"], "CMASK"], writes=[("QT", 1, blk)])
            else:
                S.op("act", lambda: nc.scalar.activation(out=QT[sq_][:, t0:t0 + n], in_=t1, func=AF.Copy), reads=[B["T1k"]],
                     writes=[("QT", sq_, blk)])

        def run_chains(items):
            for a in range(0, len(items), 2):
                gens = [qk_chain(*it, st) for st, it in enumerate(items[a:a + 2])]
                live = list(gens)
                while live:
                    nxt = []
                    for gq in live:
                        try:
                            next(gq)
                            nxt.append(gq)
                        except StopIteration:
                            pass
                    live = nxt

        qblks = ([0] if need_ctx else []) + [1, 2, 3, 4]
        pend = []
        for g in range(ngroups):
            if diff:
                q0, qn, k0, v0 = g * 128, 128, 1024 + g * 128, 2048 + g * 128
            else:
                q0, qn, k0, v0 = g * 512, 512, 1024 + g * 128, 1280 + g * 128
            S.dma_split("pool", WQ[:, :, 0:qn], w_in[:, q0:q0 + qn].rearrange("(k p) n -> p k n", p=128), "WQ")
            S.dma_split("pool", WK, w_in[:, k0:k0 + 128].rearrange("(k p) n -> p k n", p=128), "WK")
            S.dma_split("pool", WV, w_in[:, v0:v0 + 128].rearrange("(k p) n -> p k n", p=128), "WV")
            for v4 in range((NT + 3) // 4):
                cnt = min(4, NT - v4 * 4)
                bi = 4 if v4 % 2 == 0 else 0
                pb, pbk = self.P[bi], ("P", bi)
                for jj in range(cnt):
                    tg = v4 * 4 + jj
                    blk = self.blk_of_tile(tg)
                    for kc in range(KC):
                        S.op("pe", lambda: nc.tensor.matmul(pb[:, jj * 128:(jj + 1) * 128], self.HT[:, kc, tg * 128:(tg + 1) * 128], WV[:, kc, :],
                                                            start=(kc == 0), stop=(kc == KC - 1)),
                             reads=["WV", ("HT", kc, blk)], writes=[pbk])
                src = pb[:, 0:cnt * 128].rearrange("p (j e) -> p j e", e=128)
                vk = [("V", v4 * 4 + jj) for jj in range(cnt)]
                if v4 % 2 == 0:
                    S.op("dve", lambda: nc.vector.tensor_copy(out=V[:, v4 * 4:v4 * 4 + cnt, 0:128], in_=src), reads=[pbk], writes=vk)
                else:
                    S.op("act", lambda: nc.scalar.activation(out=V[:, v4 * 4:v4 * 4 + cnt, 0:128], in_=src, func=AF.Copy), reads=[pbk], writes=vk)
            while pend:
                pend.pop(0)()
            items = [("k", blk, 0) for blk in range(5)] + [("q", blk, sq_) for sq_ in range(nqw) for blk in qblks]
            run_chains(items)
            sets = [[0, 1]] if diff else [[0], [1], [2], [3]]
            PO = [(p6, ("P", 6)), (self.PTB[:].bitcast(F32), ("P", 7))]
            PTS = [(PT[:, 0, :], ("PT", 0)), (PT[:, 1, :], ("PT", 1)), (T1.bitcast(BF16)[:, 0:512], "T1")]
            for sset in sets:
                head = g if diff else g * 4 + sset[0]
                wi = self.woh_cnt % 2
                self.woh_cnt += 1
                WOHb = WOH2[:, wi, :]
                S.dma("pool", WOHb, w_out[head * 128:(head + 1) * 128, :], writes=[("WOH", wi)])
                for blk in qblks:
                    t0, n = BLKS[blk]
                    col = 1 if blk == 0 else 0
                    nq = n // 128
                    kts = [0, 1] if blk == 0 else list(range(NT))
                    nk = len(kts)
                    for s_ in sset:
                        aset = self.acc_unit % 2
                        self.acc_unit += 1
                        ACC = self.PACC[aset]
                        akeys = [("P", 2 + 2 * aset), ("P", 3 + 2 * aset)][0:(nq + 1) // 2]
                        for step in range(nk + 2):
                            if step < nk:
                                ki, kt = step, kts[step]
                                ps = self.P[ki % 2]
                                ptb_, ptk = PTS[ki % 3]
                                S.op("pe", lambda: nc.tensor.matmul(ps[:, 0:n], KT[:, kt * 128:(kt + 1) * 128], QT[s_][:, t0:t0 + n], start=True, stop=True),
                                     reads=[("KT", self.blk_of_tile(kt)), ("QT", s_, blk)], writes=[("P", ki % 2)])
                                S.op("act", lambda: nc.scalar.activation(out=ptb_[:, 0:n], in_=ps[:, 0:n], func=AF.Exp, scale=scale, bias=MISC[:, 24:25]),
                                     reads=[("P", ki % 2), "NEGC"], writes=[ptk])
                            if step >= 2:
                                ki, kt = step - 2, kts[step - 2]
                                ptb_, ptk = PTS[ki % 3]
                                for qt in range(nq):
                                    acc = ACC[:, qt * 256:qt * 256 + 129]
                                    S.op("pe", lambda: nc.tensor.matmul(acc, ptb_[:, qt * 128:(qt + 1) * 128], V[:, kt, 0:129],
                                                                        start=(ki == 0 and qt % 2 == 0), stop=(ki == nk - 1),
                                                                        skip_group_check=True),
                                         reads=[ptk, ("V", kt), "Vones"], writes=[("P", 2 + 2 * aset + qt // 2)])
                            if step >= 7 and pend:
                                pend.pop(0)()
                        if s_ == sset[-1]:
                            while pend:
                                pend.pop(0)()
                        A4 = ACC.rearrange("p (q c) -> p q c", c=256)[:, 0:nq, :]
                        R = MISC[:, 16:16 + nq]
                        S.op("dve", lambda: nc.vector.reciprocal(out=R.unsqueeze(2), in_=A4[:, :, 128:129]), reads=akeys, writes=["R"])
                        have_ob = False
                        if diff and s_ == 0:
                            S.op("dve", lambda: nc.vector.tensor_tensor(out=O1N[:, 0:nq, :], in0=A4[:, :, 0:128],
                                                                        in1=R.unsqueeze(2).to_broadcast([128, nq, 128]), op=mul),
                                 reads=akeys + ["R"], writes=["O1N"])
                        elif diff:
                            W3 = WW[:, 0:nq * 128].rearrange("p (q e) -> p q e", e=128)
                            T3 = T2[:, 0:nq * 128].rearrange("p (q e) -> p q e", e=128)
                            O3 = OB[:, 0:nq * 128].rearrange("p (q e) -> p q e", e=128)
                            SSQ = MISC[:, 20:20 + nq]
                            S.op("dve", lambda: nc.vector.tensor_scalar_mul(out=R, in0=R, scalar1=MISC[:, 6:7]), reads=["R", "NLAM"], writes=["R"])
                            S.op("dve", lambda: nc.vector.tensor_tensor(out=W3, in0=A4[:, :, 0:128], in1=R.unsqueeze(2).to_broadcast([128, nq, 128]), op=mul),
                                 reads=akeys + ["R"], writes=["QF"])
                            S.op("dve", lambda: nc.vector.tensor_tensor(out=W3, in0=W3, in1=O1N[:, 0:nq, :], op=add), reads=["QF", "O1N"], writes=["QF"])
                            S.op("dve", lambda: nc.vector.tensor_tensor(out=T3, in0=W3, in1=W3, op=mul), reads=["QF"], writes=["T2"])
                            S.op("dve", lambda: nc.vector.reduce_sum(out=SSQ, in_=T3, axis=AX.X), reads=["T2"], writes=["SSQ"])
                            S.op("act", lambda: nc.scalar.activation(out=SSQ, in_=SSQ, func=AF.Ln, scale=1.0 / 128, bias=self.epsT[:, 0:1]),
                                 reads=["SSQ", "epsT"], writes=["SSQ"])
                            S.op("act", lambda: nc.scalar.activation(out=SSQ, in_=SSQ, func=AF.Exp, scale=-0.5), reads=["SSQ"], writes=["SSQ"])
                            S.op("dve", lambda: nc.vector.tensor_tensor(out=W3, in0=W3, in1=SSQ.unsqueeze(2).to_broadcast([128, nq, 128]), op=mul),
                                 reads=["QF", "SSQ"], writes=["QF"])
                            S.op("dve", lambda: nc.vector.tensor_tensor(out=O3, in0=W3, in1=SGB.unsqueeze(1).to_broadcast([128, nq, 128]), op=mul),
                                 reads=["QF", "SGB"], writes=["RSQ"])
                            have_ob = True
                        else:
                            O3 = OB[:, 0:nq * 128].rearrange("p (q e) -> p q e", e=128)
                            S.op("dve", lambda: nc.vector.tensor_tensor(out=O3, in0=A4[:, :, 0:128], in1=R.unsqueeze(2).to_broadcast([128, nq, 128]), op=mul),
                                 reads=akeys + ["R"], writes=["RSQ"])
                            have_ob = True
                        if have_ob:
                            def mk_tr(t0=t0, n=n, nq=nq, blk=blk):
                                def f():
                                    for qt in range(nq):
                                        S.op("pe", lambda: nc.tensor.transpose(self.PTB[:, qt * 128:(qt + 1) * 128], OB[:, qt * 128:(qt + 1) * 128], self.identb[:]),
                                             reads=["RSQ", "identb"], writes=[("P", 7)])
                                    S.op("dve", lambda: nc.vector.tensor_copy(out=OT[:, t0:t0 + n], in_=self.PTB[:, 0:n]), reads=[("P", 7)], writes=[("OT", blk)])
                                return f

                            def mk_op(oc, t0=t0, n=n, blk=blk, col=col, wi=wi, WOHb=WOHb):
                                def f():
                                    pp, ppk = PO[oc % 2]
                                    S.op("pe", lambda: nc.tensor.matmul(pp[:, 0:n], WOHb[:, oc * 128:(oc + 1) * 128], OT[:, t0:t0 + n], start=True, stop=True),
                                         reads=[("WOH", wi), ("OT", blk)], writes=[ppk])
                                    xs = self.XT[:, oc, t0:t0 + n]
                                    S.op("dve", lambda: nc.vector.scalar_tensor_tensor(out=xs, in0=pp[:, 0:n], scalar=self.MOD[:, 2, oc, col:col + 1], in1=xs,
                                                                                       op0=mul, op1=add),
                                         reads=[ppk, self.modkey + (2,), ("XT", oc, blk)], writes=[("XT", oc, blk)])
                                return f

                            pend = [mk_tr()] + [mk_op(oc) for oc in range(KC)]
        while pend:
            pend.pop(0)()

    def emit_hgrn(self, i):
        nc, S, I = self.nc, self.S, self.I
        w_in, w_out = I["hgrn_w_in"], I["hgrn_w_out"]
        mul, add, sub = ALU.mult, ALU.add, ALU.subtract
        NCH = T // 64
        BT = []
        for o_ in (0, 3584):
            BT.append(dict(SIG=self.scr(o_, 1024, F32), PRE=self.scr(o_ + 1024, 1024, F32), EX=self.scr(o_ + 2048, 1024, F32),
                           KKb=self.scr(o_ + 3072, 512)))
        QTLd = [self.scr(7168, 2304), self.scr(11776, 2304)]
        KTLd = [self.scr(9472, 2304), self.scr(14080, 2304)]
        KTOKG = self.scr(16384, 2048)[0:64].rearrange("p (b j k) -> p b j k", b=2, j=8)
        EPN = self.scr(18432, 4608).rearrange("p (n v) -> p n v", n=NCH)
        S32P = self.scr(23040, 512, F32).rearrange("p (b v) -> p b v", b=2)
        SQO = self.scr(4608, 9216, F32)[0:64].rearrange("p (j v) -> p j v", j=NCH)
        GS = self.scr(0, 4608)[0:64].rearrange("p (j v) -> p j v", j=NCH)
        OT = self.scr(13824, 2304)
        OACC = self.big(0, 9216, F32)[0:64].rearrange("p (j v) -> p j v", j=NCH)
        V = self.big(9216, 4608)[0:64].rearrange("p (j v) -> p j v", j=NCH)
        KKF = self.ARENA[:, self.BIGOFF + 13824:self.BIGOFF + 18432]
        QS = KKF[:, 0:2304]
        SCG = KKF[0:64, 2304:3328].rearrange("p (b j t) -> p b j t", b=2, j=8)
        OB = GS
        SQF = self.SQ[:].rearrange("p c n -> p (c n)")
        W = SQF[:, 0:2048].rearrange("p (b k n) -> p b k n", b=2, k=KC)
        WOH = SQF[:, 2048:3072]
        S32 = SQF[:, 3072:3328].bitcast(F32)
        MSK = SQF[0:64, 3328:3584].bitcast(F32).rearrange("p (d t) -> p d t", d=2)
        GB = SQF[0:64, 3584:3840].bitcast(F32)
        LBR = SQF[:, 3840:3968].bitcast(F32)
        TF = self.TMPF[:].rearrange("p a n -> p (a n)")
        LBS = TF[:, 0:16].rearrange("p (d c) -> p d c", d=2)
        LBV = TF[:, 16:32].rearrange("p (d c) -> p d c", d=2)
        OML = TF[:, 32:48].rearrange("p (d c) -> p d c", d=2)
        NOML = TF[:, 48:64].rearrange("p (d c) -> p d c", d=2)
        EBLd = [TF[:, 64:64 + NCH], TF[:, 320:320 + NCH]]
        CM512 = TF[:, 512:768].bitcast(BF16)
        SSQ = TF[0:64, 128:128 + NCH]
        RSTD = TF[0:64, 192:192 + NCH]
        EBLNB = TF[:, 256:256 + NCH]

        if not self.norm_done:
            self.norm_mod_all(list(range(5)), 1, lambda blk, c: (self.HT[:, c, BLKS[blk][0]:BLKS[blk][0] + BLKS[blk][1]], ("HT", c, blk)))
        self.norm_done = False
        S.barrier()

        S.dma("sp", CM512, I["cmscan"][:, 0:512], writes=["CM"])
        S.dma("sp", MSK, I["hmask"], writes=["MSK"])
        GCOL = TF[:, 768:769]
        self.load_fm(GCOL, I["hgrn_gain"].rearrange("(o v) -> o v", o=1), 1, "GCOL")
        self.load_fm(LBR, I["hgrn_lb"], 64, "LBR")
        S.op("act", lambda: nc.scalar.activation(out=LBR, in_=LBR, func=AF.Exp), reads=["LBR"], writes=["LBR"])
        E4 = LBR.rearrange("p (d l c) -> p d l c", d=2, l=4)
        S.op("dve", lambda: nc.vector.tensor_tensor(out=LBS, in0=E4[:, :, 0, :], in1=E4[:, :, 1, :], op=add), reads=["LBR"], writes=["LBS"])
        S.op("dve", lambda: nc.vector.tensor_tensor(out=LBS, in0=LBS, in1=E4[:, :, 2, :], op=add), reads=["LBR", "LBS"], writes=["LBS"])
        S.op("dve", lambda: nc.vector.tensor_tensor(out=LBS, in0=LBS, in1=E4[:, :, 3, :], op=add), reads=["LBR", "LBS"], writes=["LBS"])
        S.op("dve", lambda: nc.vector.reciprocal(out=LBS, in_=LBS), reads=["LBS"], writes=["LBS"])
        S.op("dve", lambda: nc.vector.tensor_copy(out=LBV, in_=E4[:, :, 1, :]), reads=["LBR"], writes=["LBV"])
        for l in range(2, i + 1):
            S.op("dve", lambda: nc.vector.tensor_tensor(out=LBV, in0=LBV, in1=E4[:, :, l, :], op=add), reads=["LBR", "LBV"], writes=["LBV"])
        S.op("dve", lambda: nc.vector.tensor_tensor(out=LBV, in0=LBV, in1=LBS, op=mul), reads=["LBV", "LBS"], writes=["LBV"])
        S.op("dve", lambda: nc.vector.tensor_scalar(out=OML, in0=LBV, scalar1=-1.0, scalar2=1.0, op0=mul, op1=add), reads=["LBV"], writes=["OML"])
        S.op("dve", lambda: nc.vector.tensor_scalar_add(out=NOML, in0=LBV, scalar1=-1.0), reads=["LBV"], writes=["NOML"])

        p6 = self.P[6]
        wcnt = [0]

        def load_w(col0):
            b = wcnt[0] % 2
            wcnt[0] += 1
            S.dma_split("pool", W[:, b], w_in[:, col0:col0 + 128].rearrange("(k p) n -> p k n", p=128), ("W", b))
            return b

        def proj_fm(b, blk):
            t0, n = BLKS[blk]
            for kc in range(KC):
                S.op("pe", lambda: nc.tensor.matmul(p6[:, 0:n], W[:, b, kc, :], self.HT[:, kc, t0:t0 + n], start=(kc == 0), stop=(kc == KC - 1)),
                     reads=[("W", b), ("HT", kc, blk)], writes=[("P", 6)])

        def proj_tm(b, j4, pbank, pkey):
            for jj in range(4):
                j = j4 * 4 + jj
                blk = self.blk_of_tile(j // 2)
                for kc in range(KC):
                    S.op("pe", lambda: nc.tensor.matmul(pbank[0:64, jj * 128:(jj + 1) * 128], self.HT[:, kc, j * 64:(j + 1) * 64], W[:, b, kc, :],
                                                        start=(kc == 0), stop=(kc == KC - 1)),
                         reads=[("W", b), ("HT", kc, blk)], writes=[pkey])

        def rr(gens):
            live = list(gens)
            while live:
                nxt = []
                for gq in live:
                    try:
                        next(gq)
                        nxt.append(gq)
                    except StopIteration:
                        pass
                live = nxt

        def p1_block(d, h, blk, st, wb):
            t0, n = BLKS[blk]
            ch0, nch = t0 // 64, n // 64
            B = BT[st]
            pz, pzk = (self.P[6], ("P", 6)) if st == 0 else (self.P[5], ("P", 5))
            sig, pre, ex, kkb = B["SIG"][:, 0:n], B["PRE"][:, 0:n], B["EX"][:, 0:n], B["KKb"][:, 0:n]
            ks, kp, ke, kk_ = ("SIG", st), ("PRE", st), ("EX", st), ("KKb", st)
            for kc in range(KC):
                S.op("pe", lambda: nc.tensor.matmul(pz[:, 0:n], W[:, wb, kc, :], self.HT[:, kc, t0:t0 + n], start=(kc == 0), stop=(kc == KC - 1)),
                     reads=[("W", wb), ("HT", kc, blk)], writes=[pzk])
            yield
            S.op("act", lambda: nc.scalar.activation(out=sig, in_=pz[:, 0:n], func=AF.Sigmoid), reads=[pzk], writes=[ks])
            yield
            S.op("dve", lambda: nc.vector.tensor_scalar(out=kkb, in0=sig, scalar1=NOML[:, d, h:h + 1], scalar2=OML[:, d, h:h + 1], op0=mul, op1=add),
                 reads=[ks, "NOML", "OML"], writes=[kk_])
            S.op("act", lambda: nc.scalar.activation(out=sig, in_=sig, func=AF.Ln, scale=OML[:, d, h:h + 1], bias=LBV[:, d, h:h + 1]),
                 reads=[ks, "OML", "LBV"], writes=[ks])
            yield
            S.op("dve", lambda: nc.vector.tensor_tensor_scan(out=pre, data0=CM512[:, 0:n], data1=sig, initial=0.0, op0=mul, op1=add),
                 reads=[ks, "CM"], writes=[kp])
            yield
            pre3 = pre.rearrange("p (j s) -> p j s", s=64)
            S.op("act", lambda: nc.scalar.activation(out=EBLd[d][:, ch0:ch0 + nch], in_=pre3[:, :, 63], func=AF.Exp), reads=[kp],
                 writes=[("EBL", d, blk)])
            if d == 0:
                bc, kb_, ex2, ke2 = pre, kp, sig, ks
            else:
                sig3 = sig.rearrange("p (j s) -> p j s", s=64)
                S.op("dve", lambda: nc.vector.tensor_tensor(out=sig, in0=sig, in1=pre, op=sub), reads=[ks, kp], writes=[ks])
                S.op("dve", lambda: nc.vector.tensor_tensor(out=sig3, in0=sig3, in1=pre3[:, :, 63:64].to_broadcast([128, nch, 64]), op=add),
                     reads=[ks, kp], writes=[ks])
                bc, kb_, ex2, ke2 = sig, ks, pre, kp
            S.op("dve", lambda: nc.vector.tensor_scalar_max(out=bc, in0=bc, scalar1=-80.0), reads=[kb_, ("EBL", d, blk)], writes=[kb_])
            yield
            S.op("act", lambda: nc.scalar.activation(out=ex, in_=bc, func=AF.Exp), reads=[kb_], writes=[ke])
            S.op("act", lambda: nc.scalar.activation(out=ex2, in_=bc, func=AF.Exp, scale=-1.0), reads=[kb_, ("EBL", d, blk)], writes=[ke2])
            yield
            alias = ["OT"] if d == 1 else []
            S.op("dve", lambda: nc.vector.tensor_tensor(out=QTLd[d][:, t0:t0 + n], in0=QS[:, t0:t0 + n], in1=ex, op=mul),
                 reads=[("QS", blk), ke], writes=[("QTL", d, blk)] + alias)
            S.op("dve", lambda: nc.vector.tensor_tensor(out=KTLd[d][:, t0:t0 + n], in0=kkb, in1=ex2, op=mul),
                 reads=[kk_, ke2], writes=[("KTL", d, blk)] + alias)

        def phase1(d, h, wb):
            for a_ in range(0, 5, 2):
                live = [p1_block(d, h, blk, st, wb) for st, blk in enumerate(range(a_, min(a_ + 2, 5)))]
                while live:
                    nxt = []
                    for gq in live:
                        try:
                            next(gq)
                            nxt.append(gq)
                        except StopIteration:
                            pass
                    live = nxt
                    yield

        def phase2(d):
            QTL, KTL, EBL = QTLd[d], KTLd[d], EBLd[d]
            order = list(range(NCH)) if d == 0 else [3, 2, 1, 0] + list(range(NCH - 1, 3, -1))
            npos = {j: n_ for n_, j in enumerate(order)}
            eblk = [("EBL", d, blk) for blk in range(5)]
            cb = lambda j: self.blk_of_tile(j // 2)
            if d == 0:
                EBLN = EBL
            else:
                EBLN = EBLNB
                S.op("act", lambda: nc.scalar.activation(out=EBLNB[:, 0:4], in_=EBL[:, 3::-1], func=AF.Copy), reads=eblk, writes=["EBLN"])
                S.op("act", lambda: nc.scalar.activation(out=EBLNB[:, 4:NCH], in_=EBL[:, NCH - 1:3:-1], func=AF.Copy), reads=eblk, writes=["EBLN"])
            for ng in range(5):
                cnt = min(8, NCH - ng * 8)
                kb = ng % 2
                for jj in range(cnt):
                    j = order[ng * 8 + jj]
                    S.op("pe", lambda: nc.tensor.transpose(self.PTB[0:64, jj * 128:(jj + 1) * 128], KTL[:, j * 64:(j + 1) * 64], self.identb[:]),
                         reads=[("KTL", d, cb(j)), "identb"], writes=[("P", 7)])
                S.op("act", lambda: nc.scalar.activation(out=KTOKG[:, kb, 0:cnt, :], in_=self.PTB[0:64, 0:cnt * 128].rearrange("p (j k) -> p j k", k=128),
                                                          func=AF.Copy), reads=[("P", 7)], writes=[("KTOKG", kb)])
                yield
                for half in range((cnt + 3) // 4):
                    pb, pk = self.P[4], ("P", 4)
                    n0 = ng * 8 + half * 4
                    c4 = min(4, NCH - n0)
                    for jj in range(c4):
                        j = order[n0 + jj]
                        S.op("pe", lambda: nc.tensor.matmul(pb[:, jj * 128:(jj + 1) * 128], KTOKG[:, kb, half * 4 + jj, :], V[:, j, :], start=True, stop=True),
                             reads=[("KTOKG", kb), ("V", j // 4)], writes=[pk])
                    S.op("dve", lambda: nc.vector.tensor_tensor(out=EPN[:, n0:n0 + c4, :],
                                                                in0=pb[:, 0:c4 * 128].rearrange("p (n v) -> p n v", v=128),
                                                                in1=EBLN[:, n0:n0 + c4].unsqueeze(2).to_broadcast([128, c4, 128]), op=mul),
                         reads=[pk, "EBLN"] + eblk, writes=[("EP", n0 + q_) for q_ in range(c4)])
                    yield
            for n_ in range(NCH - 1):
                cur, prv = S32P[:, n_ % 2, :], S32P[:, (n_ + 1) % 2, :]
                if n_ == 0:
                    S.op("dve", lambda: nc.vector.tensor_copy(out=cur, in_=EPN[:, 0, :]), reads=[("EP", 0)], writes=[("S32", 0)])
                else:
                    S.op("dve", lambda: nc.vector.scalar_tensor_tensor(out=cur, in0=prv, scalar=EBLN[:, n_:n_ + 1], in1=EPN[:, n_, :],
                                                                       op0=mul, op1=add),
                         reads=[("S32", (n_ + 1) % 2), ("EP", n_), "EBLN"] + eblk, writes=[("S32", n_ % 2)])
                    S.op("act", lambda: nc.scalar.activation(out=EPN[:, n_, :], in_=cur, func=AF.Copy), reads=[("S32", n_ % 2)], writes=[("EP", n_)])
                if n_ % 3 == 2:
                    yield
            for jg in range(5):
                cnt = min(8, NCH - jg * 8)
                sb_ = jg % 2
                ps, psk = self.P[sb_], ("P", sb_)
                for jj in range(cnt):
                    j = jg * 8 + jj
                    c0 = j * 64
                    S.op("pe", lambda: nc.tensor.matmul(ps[0:64, jj * 64:(jj + 1) * 64], KTL[:, c0:c0 + 64], QTL[:, c0:c0 + 64], start=True, stop=True),
                         reads=[("KTL", d, cb(j)), ("QTL", d, cb(j))], writes=[psk])
                S.op("dve", lambda: nc.vector.tensor_tensor(out=SCG[:, sb_, 0:cnt, :], in0=ps[0:64, 0:cnt * 64].rearrange("p (j t) -> p j t", t=64),
                                                            in1=MSK[:, d, :].unsqueeze(1).to_broadcast([64, cnt, 64]), op=mul),
                     reads=[psk, "MSK"], writes=[("SCG", sb_)])
                yield
                for half in range((cnt + 3) // 4):
                    bi = 2 + (jg * 2 + half) % 2
                    po, pok = self.P[bi], ("P", bi)
                    j0 = jg * 8 + half * 4
                    c4 = min(4, NCH - j0)
                    for jj in range(c4):
                        j = j0 + jj
                        n_ = npos[j]
                        c0 = j * 64
                        S.op("pe", lambda: nc.tensor.matmul(po[0:64, jj * 128:(jj + 1) * 128], SCG[:, sb_, half * 4 + jj, :], V[:, j, :],
                                                            start=True, stop=(n_ == 0)),
                             reads=[("SCG", sb_), ("V", j // 4)], writes=[pok])
                        if n_ > 0:
                            S.op("pe", lambda: nc.tensor.matmul(po[0:64, jj * 128:(jj + 1) * 128], QTL[:, c0:c0 + 64], EPN[:, n_ - 1, :],
                                                                start=False, stop=True),
                                 reads=[("QTL", d, cb(j)), ("EP", n_ - 1)], writes=[pok])
                    src = po[0:64, 0:c4 * 128].rearrange("p (j v) -> p j v", v=128)
                    ok_ = [("OACC", j0 + jj) for jj in range(c4)]
                    if d == 0:
                        S.op("act", lambda: nc.scalar.activation(out=OACC[:, j0:j0 + c4, :], in_=src, func=AF.Copy), reads=[pok], writes=ok_)
                    else:
                        S.op("dve", lambda: nc.vector.tensor_tensor(out=OACC[:, j0:j0 + c4, :], in0=OACC[:, j0:j0 + c4, :], in1=src, op=add),
                             reads=[pok] + ok_, writes=ok_)
                    yield

        WOH2 = [WOH, self.RS[:].rearrange("p a n -> p (a n)").bitcast(BF16)]
        btkeys = [(nm, st_) for nm in ("SIG", "PRE", "EX", "KKb") for st_ in range(2)]
        oak = [("OACC", j) for j in range(NCH)]

        def qv_gen(h):
            S.dma("pool", WOH2[h % 2], w_out[h * 128:(h + 1) * 128, :], writes=[("WOH", h % 2)])
            S.op("act", lambda: nc.scalar.activation(out=WOH2[h % 2], in_=WOH2[h % 2], func=AF.Copy, scale=GCOL), reads=[("WOH", h % 2), "GCOL"],
                 writes=[("WOH", h % 2)])
            b = load_w(h * 128)
            for blk in range(5):
                t0, n = BLKS[blk]
                proj_fm(b, blk)
                S.op("act", lambda: nc.scalar.activation(out=QS[:, t0:t0 + n], in_=p6[:, 0:n], func=AF.Silu), reads=[("P", 6)], writes=[("QS", blk)])
                yield
            b = load_w(3072 + h * 128)
            for j4 in range(NCH // 4):
                pb, pk = self.P[4 + j4 % 2], ("P", 4 + j4 % 2)
                proj_tm(b, j4, pb, pk)
                src = pb[0:64, :].rearrange("p (j v) -> p j v", j=4)
                if j4 % 2 == 0:
                    S.op("dve", lambda: nc.vector.tensor_copy(out=V[:, j4 * 4:(j4 + 1) * 4, :], in_=src), reads=[pk], writes=[("V", j4)])
                else:
                    S.op("act", lambda: nc.scalar.activation(out=V[:, j4 * 4:(j4 + 1) * 4, :], in_=src, func=AF.Copy), reads=[pk], writes=[("V", j4)])
                yield

        def gproj_gen(h):
            b = load_w(4096 + h * 128)
            for j4 in range(NCH // 4):
                bi = 5 + j4 % 2
                pb, pk = self.P[bi], ("P", bi)
                proj_tm(b, j4, pb, pk)
                src = pb[0:64, :].rearrange("p (j v) -> p j v", j=4)
                S.op("act", lambda: nc.scalar.activation(out=GS[:, j4 * 4:(j4 + 1) * 4, :], in_=src, func=AF.Silu), reads=[pk],
                     writes=["GS"] + btkeys)
                yield

        def finish_gen(h):
            S.op("act", lambda: nc.scalar.activation(out=SQO, in_=OACC, func=AF.Square), reads=oak, writes=["SQO"])
            yield
            S.op("dve", lambda: nc.vector.reduce_sum(out=SSQ, in_=SQO, axis=AX.X), reads=["SQO"], writes=["SSQ"])
            S.op("act", lambda: nc.scalar.activation(out=RSTD, in_=SSQ, func=AF.Ln, scale=1.0 / 128, bias=self.epsT[0:64, 0:1]), reads=["SSQ", "epsT"],
                 writes=["RSTD"])
            S.op("act", lambda: nc.scalar.activation(out=RSTD, in_=RSTD, func=AF.Exp, scale=-0.5), reads=["RSTD"], writes=["RSTD"])
            yield
            S.op("dve", lambda: nc.vector.tensor_tensor(out=OACC, in0=OACC, in1=RSTD.unsqueeze(2).to_broadcast([64, NCH, 128]), op=mul),
                 reads=oak + ["RSTD"], writes=oak)
            yield
            S.op("dve", lambda: nc.vector.tensor_tensor(out=OB, in0=OACC, in1=GS, op=mul), reads=oak + ["GS"], writes=["GS"])
            yield
            done = 0
            while done < NCH:
                cnt = min(16, NCH - done)
                for jj in range(cnt):
                    S.op("pe", lambda: nc.tensor.transpose(self.PTB[:, jj * 64:(jj + 1) * 64], OB[:, done + jj, :], self.identb[0:64, 0:64]),
                         reads=["GS", "identb"], writes=[("P", 7)])
                S.op("dve", lambda: nc.vector.tensor_copy(out=OT[:, done * 64:(done + cnt) * 64], in_=self.PTB[:, 0:cnt * 64]),
                     reads=[("P", 7)], writes=["OT"])
                done += cnt
                yield

        def outproj_gen(h):
            wi = h % 2
            for blk in range(5):
                t0, n = BLKS[blk]
                col = 1 if blk == 0 else 0
                for oc in range(KC):
                    pp, ppk = self.P[oc % 2], ("P", oc % 2)
                    S.op("pe", lambda: nc.tensor.matmul(pp[:, 0:n], WOH2[wi][:, oc * 128:(oc + 1) * 128], OT[:, t0:t0 + n], start=True, stop=True),
                         reads=[("WOH", wi), "OT"], writes=[ppk])
                    xs = self.XT[:, oc, t0:t0 + n]
                    S.op("dve", lambda: nc.vector.scalar_tensor_tensor(out=xs, in0=pp[:, 0:n], scalar=self.MOD[:, 2, oc, col:col + 1], in1=xs,
                                                                       op0=mul, op1=add),
                         reads=[ppk, self.modkey + (2,), ("XT", oc, blk)], writes=[("XT", oc, blk)])
                    if oc % 2 == 1:
                        yield

        rr([qv_gen(0)])
        for h in range(8):
            S.barrier()
            wb0 = load_w(1024 + h * 128)
            rr([phase1(0, h, wb0)] + ([outproj_gen(h - 1)] if h > 0 else []))
            wb1 = load_w(2048 + h * 128)
            rr([phase1(1, h, wb1), phase2(0)])
            rr([phase2(1), gproj_gen(h)])
            S.barrier()
            rr([finish_gen(h)] + ([qv_gen(h + 1)] if h < 7 else []))
        S.barrier()
        rr([outproj_gen(7)])

    def emit_ffn(self, i):
        nc, S, I = self.nc, self.S, self.I
        blks = list(range(5)) if i < DEPTH - 1 else list(range(1, 5))
        WG = self.scr(0, 2 * KC * 256).rearrange("p (b k n) -> p b k n", b=2, k=KC)
        WU = self.scr(4096, 2 * KC * 256).rearrange("p (b k n) -> p b k n", b=2, k=KC)
        WO = self.scr(8192, 8 * 1024).rearrange("p (j n) -> p j n", j=8)
        SG = self.scr(16384, 2 * 512 * 2, F32).rearrange("p (b n) -> p b n", b=2)
        ACTT = self.big(0, 8 * T).rearrange("p (j t) -> p j t", j=8)
        self.adaln_finish(i, 2)
        self.norm_mod_all(blks, 2, lambda blk, c: (self.HT[:, c, BLKS[blk][0]:BLKS[blk][0] + BLKS[blk][1]], ("HT", c, blk)))
        wcount = 0
        sgi = 0
        pgi = 0
        ada_i = i + 1 if i + 1 < self.n_layers else None
        self.ada_next = 0
        for (j0, j1) in THIRDS:
            for jp in range(j0, j1, 2):
                b = wcount % 2
                wcount += 1
                S.dma_split("pool", WG[:, b], I["ffn_w_in"][i, :, jp * 128:jp * 128 + 256].rearrange("(k p) n -> p k n", p=128), ("WG", b))
                S.dma_split("pool", WU[:, b], I["ffn_w_in"][i, :, DFF + jp * 128:DFF + jp * 128 + 256].rearrange("(k p) n -> p k n", p=128), ("WU", b))
                for jj in range(2):
                    j = jp + jj
                    for blk in blks:
                        t0, n = BLKS[blk]
                        if ada_i is not None and blk in (1, 3):
                            if self.ada_next < 24:
                                self.adaln_block(ada_i, self.ada_next)
                                self.ada_next += 1
                            else:
                                self.adaln_flush()
                        pg = self.P[pgi % 2]
                        pu = self.P[2 + pgi % 2]
                        kg, ku = ("P", pgi % 2), ("P", 2 + pgi % 2)
                        pgi += 1
                        hk = [("HT", c, blk) for c in range(KC)]
                        for kc in range(KC):
                            S.op("pe", lambda: nc.tensor.matmul(pg[:, 0:n], WG[:, b, kc, jj * 128:(jj + 1) * 128], self.HT[:, kc, t0:t0 + n],
                                                                start=(kc == 0), stop=(kc == KC - 1)),
                                 reads=[("WG", b), hk[kc]], writes=[kg])
                        for kc in range(KC):
                            S.op("pe", lambda: nc.tensor.matmul(pu[:, 0:n], WU[:, b, kc, jj * 128:(jj + 1) * 128], self.HT[:, kc, t0:t0 + n],
                                                                start=(kc == 0), stop=(kc == KC - 1)),
                                 reads=[("WU", b), hk[kc]], writes=[ku])
                        sg = SG[:, sgi % 2, 0:n]
                        sk = ("SG", sgi % 2)
                        sgi += 1
                        S.op("act", lambda: nc.scalar.activation(out=sg, in_=pg[:, 0:n], func=AF.Silu), reads=[kg], writes=[sk])
                        S.op("dve", lambda: nc.vector.tensor_tensor(out=ACTT[:, j - j0, t0:t0 + n], in0=sg, in1=pu[:, 0:n], op=ALU.mult),
                             reads=[sk, ku], writes=[("ACTT", j - j0, blk)])
            nj = j1 - j0
            S.dma_split("pool", WO[:, 0:nj, :], I["ffn_w_out"][i, j0 * 128:j1 * 128, :].rearrange("(j p) n -> p j n", p=128), "WO")
            ngen = None
            if (j0, j1) == THIRDS[-1] and ada_i is not None and (ada_i % 4) != 0:
                self.adaln_finish(ada_i, 1)
                ngen = self.norm_mod_gen(list(range(5)), 1, lambda blk, c: (self.HT[:, c, BLKS[blk][0]:BLKS[blk][0] + BLKS[blk][1]], ("HT", c, blk)),
                                         MOD=self.MODS[ada_i % 2], modkey=("MOD", ada_i % 2))
                self.norm_done = True
            for blk in blks:
                if ngen is not None and blk > blks[0]:
                    next(ngen, None)
                t0, n = BLKS[blk]
                col = 1 if blk == 0 else 0
                for oc in range(KC):
                    po = self.P[4 + oc % 2]
                    pk = ("P", 4 + oc % 2)
                    for jj in range(nj):
                        S.op("pe", lambda: nc.tensor.matmul(po[:, 0:n], WO[:, jj, oc * 128:(oc + 1) * 128], ACTT[:, jj, t0:t0 + n],
                                                            start=(jj == 0), stop=(jj == nj - 1)),
                             reads=["WO", ("ACTT", jj, blk)], writes=[pk])
                    xs = self.XT[:, oc, t0:t0 + n]
                    S.op("dve", lambda: nc.vector.scalar_tensor_tensor(out=xs, in0=po[:, 0:n], scalar=self.MOD[:, 5, oc, col:col + 1], in1=xs,
                                                                       op0=ALU.mult, op1=ALU.add),
                         reads=[pk, self.modkey + (5,), ("XT", oc, blk)], writes=[("XT", oc, blk)])
                if (j0, j1) == THIRDS[-1] and i == self.n_layers - 1 and not self.debug and blk >= 1:
                    XO = self.scr(18432, 4096, F32).rearrange("p (a n) -> p a n", a=2)
                    obanks = [[(self.P[6], ("P", 6))], [(self.PTB[:].bitcast(F32), ("P", 7))]]
                    for tg in range(2 + (blk - 1) * 4, 2 + blk * 4):
                        self.output_tile(tg, XO, obanks)
                        self.out_done.add(tg)
            if ngen is not None:
                for _ in ngen:
                    pass

    def output_tile(self, tg, XIO, banks):
        nc, S = self.nc, self.S
        blk = self.blk_of_tile(tg)
        si = tg % 2
        for h in range(2):
            pb, pk = banks[h][tg % len(banks[h])]
            for cc in range(4):
                c = h * 4 + cc
                S.op("pe", lambda: nc.tensor.transpose(pb[:, cc * 128:(cc + 1) * 128], self.XT[:, c, tg * 128:(tg + 1) * 128], self.identf[:]),
                     reads=[("XT", c, blk), "identf"], writes=[pk])
            dst = XIO[:, si, h * 512:(h + 1) * 512]
            if h == 0:
                S.op("dve", lambda: nc.vector.tensor_copy(out=dst, in_=pb[:, 0:512]), reads=[pk], writes=[("STG", si, h)])
            else:
                S.op("act", lambda: nc.scalar.activation(out=dst, in_=pb[:, 0:512], func=AF.Copy), reads=[pk], writes=[("STG", si, h)])
        if self.debug:
            key = ("dbg", tg)
            S.dma("sp", self.dbg[tg * 128:(tg + 1) * 128, :], XIO[:, si, :], reads=[("STG", si, 0), ("STG", si, 1)], writes=[key])
            self.outkeys.append(key)
        if tg >= 2:
            key = ("out", tg)
            S.dma("sp", self.out[(tg - 2) * 128:(tg - 1) * 128, :], XIO[:, si, :], reads=[("STG", si, 0), ("STG", si, 1)], writes=[key])
            self.outkeys.append(key)

    def emit_output(self):
        S = self.S
        XIO = self.scr(0, 4096, F32).rearrange("p (a n) -> p a n", a=2)
        banks = [[(self.P[0], ("P", 0)), (self.P[2], ("P", 2))], [(self.P[1], ("P", 1)), (self.P[3], ("P", 3))]]
        tiles = range(NT) if self.debug else range(2, NT)
        for tg in tiles:
            if tg not in self.out_done:
                self.output_tile(tg, XIO, banks)
        S.wait_all("sp", self.outkeys)


def make_in_maps(inputs):
    cst = make_consts()
    f = lambda a: np.ascontiguousarray(np.asarray(a, dtype=np.float32))
    shared = {
        "c_ctx": f(inputs["c_ctx"]).reshape(KC, 128),
        "w_ada": f(inputs["w_ada"]),
        "b_ada": f(inputs["b_ada"]).reshape(DEPTH * 48, 128),
        "norm_gain": f(inputs["norm_gain"]).reshape(DEPTH * 2 * KC, 128),
        "ffn_w_in": f(inputs["ffn_w_in"]),
        "ffn_w_out": f(inputs["ffn_w_out"]),
        "fnet_w_out": f(inputs["fnet_w_out"]).reshape(D, D),
        "fnet_b_out": f(inputs["fnet_b_out"]).reshape(KC, 128),
        "diff_w_in": f(inputs["diff_w_in"]).reshape(D, 3 * D),
        "diff_w_out": f(inputs["diff_w_out"]).reshape(D, D),
        "diff_lambda": f(inputs["diff_lambda"]).reshape(256),
        "diff_subln": f(inputs["diff_subln_gain"]).reshape(128),
        "hgrn_w_in": f(inputs["hgrn_w_in"]).reshape(D, 5 * D),
        "hgrn_w_out": f(inputs["hgrn_w_out"]).reshape(D, D),
        "hgrn_lb": f(inputs["hgrn_lower_bound"]).reshape(64, 128),
        "hgrn_gain": f(inputs["hgrn_norm_gain"]).reshape(128),
        "gqa_w_in": f(inputs["gqa_w_in"]).reshape(D, 1536),
        "gqa_w_out": f(inputs["gqa_w_out"]).reshape(D, D),
    }
    p = np.arange(128)
    r64 = (p - p % 64) + (p % 64 + 32) % 64
    r128 = (p + 64) % 128
    dq = f(inputs["diff_q_gain"]).reshape(128)
    dk = f(inputs["diff_k_gain"]).reshape(128)
    shared["diff_g"] = np.ascontiguousarray(np.stack([dq, dq[r64], dk, dk[r64]], axis=0))
    gq = f(inputs["gqa_q_gain"]).reshape(128)
    gk = f(inputs["gqa_k_gain"]).reshape(128)
    shared["gqa_g"] = np.ascontiguousarray(np.stack([gq, gq[r128], gk, gk[r128]], axis=0))
    shared.update(cst)
    maps = []
    for b in range(8):
        m = dict(shared)
        m["x"] = f(inputs["x"][b])
        m["c"] = f(inputs["c"][b]).reshape(KC, 128)
        m["ctx"] = f(inputs["ctx"][b])
        maps.append(m)
    return maps


_NC_CACHE = {}


def kernel(**inputs):
    if "nc" not in _NC_CACHE:
        _NC_CACHE["nc"] = Builder().build()
    nc = _NC_CACHE["nc"]
    in_maps = make_in_maps(inputs)
    res = run_bass_kernel_spmd(nc, in_maps, core_ids=list(range(8)))
    return np.stack([np.asarray(r["out"], dtype=np.float32) for r in res.results], axis=0)
```

```python
import math
from contextlib import ExitStack

import numpy as np
import ml_dtypes

import concourse.bass as bass
import concourse.mybir as mybir
from concourse.bass_utils import run_bass_kernel_spmd

F32 = mybir.dt.float32
BF16 = mybir.dt.bfloat16
AF = mybir.ActivationFunctionType
ALU = mybir.AluOpType
AX = mybir.AxisListType

D = 1024
KC = 8
SEQ = 2048
CTX = 256
T = SEQ + CTX
NT = T // 128
DEPTH = 4
DFF = 2816
NJ = DFF // 128
EPS = 1e-6
BLKS = [(0, 256), (256, 512), (768, 512), (1280, 512), (1792, 512)]
THIRDS = [(0, 8), (8, 16), (16, 22)]

SAME_ENGINE_SYNC = True


class Sched:
    def __init__(self, nc, stack, n_dma_sems=12):
        self.nc = nc
        self.engs = {"pe": nc.tensor, "act": nc.scalar, "dve": nc.vector, "pool": nc.gpsimd, "sp": nc.sync}
        self.sem = {}
        self.cnt = {}
        for e in ("pe", "act", "dve", "pool"):
            self.sem[e] = stack.enter_context(nc.semaphore("s_" + e))
            self.cnt[e] = 0
        self.dsem = {}
        self.dcnt = {}
        self.drr = {}
        for q in ("sp", "pool"):
            self.dsem[q] = [stack.enter_context(nc.semaphore(f"d_{q}{i}")) for i in range(n_dma_sems)]
            self.dcnt[q] = [0] * n_dma_sems
            self.drr[q] = 0
        self.semobj = {}
        for e, s in self.sem.items():
            self.semobj[("c", e)] = s
        for q, lst in self.dsem.items():
            for i, s in enumerate(lst):
                self.semobj[("d", q, i)] = s
        self.waited = {e: {} for e in self.engs}
        self.alias = {}
        self.res = {}
        self.n_inst = 0
        self.n_wait = 0

    def _need(self, eng, deps):
        best = {}
        for ev in deps:
            if ev is None:
                continue
            sk, val, src = ev
            if src == eng and sk[0] == "c" and (eng == "pe" or not SAME_ENGINE_SYNC):
                continue
            if best.get(sk, 0) < val:
                best[sk] = val
        e = self.engs[eng]
        for sk, val in best.items():
            if self.waited[eng].get(sk, 0) >= val:
                continue
            e.wait_ge(self.semobj[sk], val)
            self.n_wait += 1
            self.waited[eng][sk] = val

    def _x(self, keys):
        out = []
        for k in keys:
            out.extend(self.alias.get(k, (k,)))
        return out

    def dma_split(self, q, out, in_, key, reads=()):
        n = out.shape[1]
        h = n // 2
        subs = ((key, "lo"), (key, "hi"))
        self.alias[key] = subs
        self.dma(q, out[:, 0:h], in_[:, 0:h], reads=reads, writes=[subs[0]])
        self.dma(q, out[:, h:n], in_[:, h:n], reads=reads, writes=[subs[1]])

    def _deps(self, reads, writes):
        reads, writes = self._x(reads), self._x(writes)
        deps = []
        for k in reads:
            st = self.res.get(k)
            if st is not None:
                deps.append(st[0])
        for k in writes:
            st = self.res.get(k)
            if st is not None:
                deps.append(st[0])
                deps.extend(st[1])
        return deps

    def _commit(self, ev, reads, writes):
        reads, writes = self._x(reads), self._x(writes)
        for k in reads:
            st = self.res.setdefault(k, [None, []])
            st[1].append(ev)
        for k in writes:
            self.res[k] = [ev, []]

    def op(self, eng, fn, reads=(), writes=()):
        self._need(eng, self._deps(reads, writes))
        ins = fn()
        self.cnt[eng] += 1
        ins.then_inc(self.sem[eng], 1)
        ev = (("c", eng), self.cnt[eng], eng)
        self._commit(ev, reads, writes)
        self.n_inst += 1
        return ev

    def dma(self, q, out, in_, reads=(), writes=(), **kw):
        i = self.drr[q]
        self.drr[q] = (i + 1) % len(self.dsem[q])
        sk = ("d", q, i)
        deps = self._deps(reads, writes)
        if self.dcnt[q][i] > 0:
            deps.append((sk, self.dcnt[q][i], "dma"))
        self._need(q, deps)
        ins = self.engs[q].dma_start(out=out, in_=in_, **kw)
        self.dcnt[q][i] += 16
        ins.then_inc(self.dsem[q][i], 16)
        ev = (sk, self.dcnt[q][i], "dma")
        self._commit(ev, reads, writes)
        self.n_inst += 1
        return ev

    def barrier(self):
        evs = [(("c", e), self.cnt[e], "x") for e in self.cnt if self.cnt[e] > 0]
        for q in self.dsem:
            for i, v in enumerate(self.dcnt[q]):
                if v > 0:
                    evs.append((("d", q, i), v, "dma"))
        for eng in self.engs:
            self._need(eng, evs)

    def wait_all(self, eng, keys):
        deps = []
        for k in self._x(keys):
            st = self.res.get(k)
            if st is not None:
                deps.append(st[0])
                deps.extend(st[1])
        self._need(eng, deps)


def _bf(a):
    return np.ascontiguousarray(a.astype(ml_dtypes.bfloat16))


_CONST_CACHE = {}


def make_consts():
    if _CONST_CACHE:
        return _CONST_CACHE
    c = {}
    c["ident_f"] = np.eye(128, dtype=np.float32)
    c["ident_b"] = _bf(np.eye(128, dtype=np.float32))
    k = np.arange(128)
    m = (k[:, None] * k[None, :]) % 128
    ang = 2 * np.pi * m / 128.0
    c["cs128"] = _bf(np.concatenate([np.cos(ang), -np.sin(ang)], axis=1) / math.sqrt(128.0))

    def dft(L):
        l = np.arange(L, dtype=np.int64)
        mm = (l[:, None] * l[None, :]) % L
        a = 2 * np.pi * mm / float(L)
        C = np.cos(a) / math.sqrt(L)
        S_ = np.sin(a) / math.sqrt(L)
        nl = L // 128
        out = np.empty((nl, 128, 2, nl, 128), dtype=np.float32)
        for z, M in enumerate((C, S_)):
            out[:, :, z] = M.reshape(nl, 128, nl, 128).transpose(2, 1, 0, 3)
        return _bf(out)

    c["dft_lat"] = dft(SEQ)
    c["dft_ctx"] = dft(CTX)

    def rope(hd):
        half = hd // 2
        nf = hd // 4
        t = np.arange(SEQ)
        row = (t // 64).astype(np.float32)
        colv = (t % 64).astype(np.float32)
        inv = (np.float32(10000.0) ** (-np.arange(nf, dtype=np.float32) / np.float32(nf))).astype(np.float32)
        ang = np.concatenate([row[:, None] * inv[None, :], colv[:, None] * inv[None, :]], axis=1).astype(np.float32)
        cosv = np.cos(ang.astype(np.float64)).astype(np.float32)
        sinv = np.sin(ang.astype(np.float64)).astype(np.float32)
        out = np.empty((128, 2, SEQ), dtype=np.float32)
        for p in range(128):
            i = p % hd
            out[p, 0] = cosv[:, i % half]
            out[p, 1] = (-1.0 if i < half else 1.0) * sinv[:, i % half]
        return np.ascontiguousarray(out)

    def rotidx(hd):
        p = np.arange(128)
        i = p % hd
        half = hd // 2
        return (p - i) + (i + half) % hd

    c["rope64"] = rope(64)
    c["rope128"] = rope(128)
    for hd in (64, 128):
        r = rotidx(hd)
        pm = np.zeros((128, 128), dtype=np.float32)
        pm[r, np.arange(128)] = 1.0
        c[f"perm{hd}"] = pm
        bo = np.zeros((128, 128), dtype=np.float32)
        for m_ in range(128):
            g0 = (m_ // hd) * hd
            bo[g0:g0 + hd, m_] = 1.0 / hd
        c[f"bo{hd}"] = _bf(bo)
    tt = np.arange(64)
    c["hmask"] = np.ascontiguousarray(np.stack([(tt[:, None] <= tt[None, :]), (tt[:, None] >= tt[None, :])], axis=1).astype(np.float32))
    cm = np.ones((128, T), dtype=np.float32)
    cm[:, 0::64] = 0.0
    c["cmscan"] = _bf(cm)
    msk = np.zeros((128, 2), dtype=np.float32)
    msk[:64, 0] = 1.0
    msk[64:, 1] = 1.0
    c["cmask"] = msk
    _CONST_CACHE.update(c)
    return c


class Builder:
    def __init__(self, n_layers=DEPTH, debug=False):
        self.n_layers = n_layers
        self.debug = debug

    def sb(self, name, shape, dt):
        return self.st.enter_context(self.nc.sbuf_tensor(name, shape, dt))

    def build(self):
        nc = bass.Bass("TRN2", target_bir_lowering=False)
        self.nc = nc
        dram = lambda n, s, d, k="ExternalInput": nc.dram_tensor(n, list(s), d, kind=k).ap()
        I = {}
        I["x"] = dram("x", [SEQ, D], F32)
        I["c"] = dram("c", [KC, 128], F32)
        I["ctx"] = dram("ctx", [CTX, D], F32)
        I["c_ctx"] = dram("c_ctx", [KC, 128], F32)
        I["w_ada"] = dram("w_ada", [DEPTH, D, 6 * D], F32)
        I["b_ada"] = dram("b_ada", [DEPTH * 48, 128], F32)
        I["norm_gain"] = dram("norm_gain", [DEPTH * 2 * KC, 128], F32)
        I["ffn_w_in"] = dram("ffn_w_in", [DEPTH, D, 2 * DFF], F32)
        I["ffn_w_out"] = dram("ffn_w_out", [DEPTH, DFF, D], F32)
        I["fnet_w_out"] = dram("fnet_w_out", [D, D], F32)
        I["fnet_b_out"] = dram("fnet_b_out", [KC, 128], F32)
        I["diff_w_in"] = dram("diff_w_in", [D, 3 * D], F32)
        I["diff_w_out"] = dram("diff_w_out", [D, D], F32)
        I["diff_g"] = dram("diff_g", [4, 128], F32)
        I["diff_lambda"] = dram("diff_lambda", [256], F32)
        I["diff_subln"] = dram("diff_subln", [128], F32)
        I["gqa_w_in"] = dram("gqa_w_in", [D, 1536], F32)
        I["gqa_w_out"] = dram("gqa_w_out", [D, D], F32)
        I["gqa_g"] = dram("gqa_g", [4, 128], F32)
        I["hgrn_w_in"] = dram("hgrn_w_in", [D, 5 * D], F32)
        I["hgrn_w_out"] = dram("hgrn_w_out", [D, D], F32)
        I["hgrn_lb"] = dram("hgrn_lb", [64, 128], F32)
        I["hgrn_gain"] = dram("hgrn_gain", [128], F32)
        I["hmask"] = dram("hmask", [64, 2, 64], F32)
        I["cmscan"] = dram("cmscan", [128, T], BF16)
        I["rope64"] = dram("rope64", [128, 2, SEQ], F32)
        I["rope128"] = dram("rope128", [128, 2, SEQ], F32)
        I["perm64"] = dram("perm64", [128, 128], F32)
        I["perm128"] = dram("perm128", [128, 128], F32)
        I["bo64"] = dram("bo64", [128, 128], BF16)
        I["bo128"] = dram("bo128", [128, 128], BF16)
        I["cmask"] = dram("cmask", [128, 2], F32)
        I["ident_f"] = dram("ident_f", [128, 128], F32)
        I["ident_b"] = dram("ident_b", [128, 128], BF16)
        I["cs128"] = dram("cs128", [128, 256], BF16)
        I["dft_lat"] = dram("dft_lat", [16, 128, 2, 16, 128], BF16)
        I["dft_ctx"] = dram("dft_ctx", [2, 128, 2, 2, 128], BF16)
        self.I = I
        self.out = dram("out", [SEQ, D], F32, "ExternalOutput")
        if self.debug:
            self.dbg = dram("dbg", [T, D], F32, "ExternalOutput")

        with ExitStack() as st:
            self.st = st
            S = Sched(nc, st)
            self.S = S
            self.XT = self.sb("XT", [128, KC, T], F32)
            self.ARENA = self.sb("ARENA", [128, 36864], BF16)
            self.HT = self.ARENA[:, 0:18432].rearrange("p (c t) -> p c t", c=KC)
            self.BIGOFF = 18432
            self.SCR = self.sb("SCR", [128, 23552], BF16)
            self.identf = self.sb("identf", [128, 128], F32)
            self.identb = self.sb("identb", [128, 128], BF16)
            self.onesb = self.sb("onesb", [128, 128], BF16)
            self.epsT = self.sb("epsT", [128, 1], F32)
            self.condT = self.sb("condT", [128, KC, 2], F32)
            self.NG = self.sb("NG", [128, DEPTH * 2 * KC], F32)
            self.BADA = self.sb("BADA", [128, DEPTH * 48], F32)
            self.MODS = [self.sb("MODa", [128, 6, KC, 2], F32), self.sb("MODb", [128, 6, KC, 2], F32)]
            self.condTb = self.sb("condTb", [128, KC, 2], BF16)
            self.STGS = self.sb("STGS", [128, 1, 128], F32)
            self.SQ = self.sb("SQ", [128, KC, 512], BF16)
            self.RS = self.sb("RS", [128, 1, 512], F32)
            self.TMPF = self.sb("TMPF", [128, 2, 512], F32)
            P01 = st.enter_context(nc.psum_tensor("P01", [128, 1024], F32))
            P23 = st.enter_context(nc.psum_tensor("P23", [128, 1024], F32))
            P45 = st.enter_context(nc.psum_tensor("P45", [128, 1024], F32))
            P6 = st.enter_context(nc.psum_tensor("P6", [128, 512], F32))
            self.P = [P01[:, 0:512], P01[:, 512:1024], P23[:, 0:512], P23[:, 512:1024], P45[:, 0:512], P45[:, 512:1024], P6[:, :]]
            self.PACC = [P23[:, :], P45[:, :]]
            self.acc_unit = 0
            self.ada_next = 0
            self.norm_done = False
            self.outkeys = []
            self.out_done = set()
            self.woh_cnt = 0
            self.PTB = st.enter_context(nc.psum_tensor("PTB", [128, 1024], BF16))
            self.tmpf_i = 0
            self.stg_i = 0

            self.emit_init()
            S.barrier()
            for i in range(self.n_layers):
                self.MOD = self.MODS[i % 2]
                self.modkey = ("MOD", i % 2)
                if not self.norm_done:
                    self.adaln_finish(i, 1)
                m = i % 4
                if m == 0:
                    self.emit_fnet(i)
                elif m == 1:
                    self.emit_attn(i, "diff")
                elif m == 3:
                    self.emit_attn(i, "gqa")
                else:
                    self.emit_hgrn(i)
                S.barrier()
                self.emit_ffn(i)
                S.barrier()
            self.emit_output()
            print("instructions", S.n_inst, "waits", S.n_wait)
        return nc

    def scr(self, off_elems, n_elems, dt=BF16):
        v = self.SCR[:, off_elems:off_elems + n_elems]
        if dt == F32:
            v = v.bitcast(F32)
        return v

    def big(self, off_elems, n_elems, dt=BF16):
        v = self.ARENA[:, self.BIGOFF + off_elems:self.BIGOFF + off_elems + n_elems]
        if dt == F32:
            v = v.bitcast(F32)
        return v

    def load_fm(self, dst, src, R, dst_key):
        nc, S = self.nc, self.S
        si = 0
        stg = self.STGS[0:R, si, 0:128]
        S.dma("sp", stg, src, writes=[("STGS", si)])
        pt = self.P[6]
        S.op("pe", lambda: nc.tensor.transpose(pt[:, 0:R], stg, self.identf[0:R, 0:R]),
             reads=[("STGS", si), "identf"], writes=[("P", 6)])
        S.op("dve", lambda: nc.vector.tensor_copy(out=dst, in_=pt[:, 0:R]), reads=[("P", 6)], writes=[dst_key])

    def emit_init(self):
        nc, S, I = self.nc, self.S, self.I
        S.dma("sp", self.identf[:], I["ident_f"], writes=["identf"])
        S.dma("sp", self.identb[:], I["ident_b"], writes=["identb"])
        S.op("dve", lambda: nc.vector.memset(self.onesb[:], 1.0 / D), writes=["onesb"])
        S.op("dve", lambda: nc.vector.memset(self.epsT[:], EPS), writes=["epsT"])
        craw = self.TMPF[:, 0, 0:16]
        self.load_fm(craw[:, 0:8], I["c"], 8, "craw0")
        self.load_fm(craw[:, 8:16], I["c_ctx"], 8, "craw1")
        S.op("act", lambda: nc.scalar.activation(out=self.condT[:, :, 0], in_=craw[:, 0:8], func=AF.Silu),
             reads=["craw0"], writes=["condT0"])
        S.op("act", lambda: nc.scalar.activation(out=self.condT[:, :, 1], in_=craw[:, 8:16], func=AF.Silu),
             reads=["craw1"], writes=["condT1"])
        S.op("dve", lambda: nc.vector.tensor_copy(out=self.condTb[:], in_=self.condT[:]), reads=["condT0", "condT1"], writes=["condTb"])
        self.load_fm(self.NG[:, 0:64], I["norm_gain"], 64, "NG")
        self.load_fm(self.BADA[:, 0:128], I["b_ada"][0:128, :], 128, "BADA0")
        self.load_fm(self.BADA[:, 128:192], I["b_ada"][128:192, :], 64, "BADA1")
        XIO = self.scr(0, 4096, F32).rearrange("p (a n) -> p a n", a=2)
        for tg in range(NT):
            src = I["ctx"][tg * 128:(tg + 1) * 128, :] if tg < 2 else I["x"][(tg - 2) * 128:(tg - 1) * 128, :]
            si = tg % 2
            S.dma("sp", XIO[:, si, :], src, writes=[("XIO", si)])
            blk = self.blk_of_tile(tg)
            for h in range(2):
                pb = self.P[h + 2 * (tg % 2)]
                pk = ("P", h + 2 * (tg % 2))
                for cc in range(4):
                    c = h * 4 + cc
                    S.op("pe", lambda: nc.tensor.transpose(pb[:, cc * 128:(cc + 1) * 128],
                                                           XIO[:, si, c * 128:(c + 1) * 128], self.identf[:]),
                         reads=[("XIO", si), "identf"], writes=[pk])
                eng = "dve" if h == 0 else "act"
                dst = self.XT[:, h * 4:(h + 1) * 4, tg * 128:(tg + 1) * 128]
                srcp = pb[:].rearrange("p (c t) -> p c t", c=4)
                wk = [("XT", h * 4 + cc, blk) for cc in range(4)]
                if eng == "dve":
                    S.op("dve", lambda: nc.vector.tensor_copy(out=dst, in_=srcp), reads=[pk], writes=wk)
                else:
                    S.op("act", lambda: nc.scalar.activation(out=dst, in_=srcp, func=AF.Copy), reads=[pk], writes=wk)
            for _ in range(2 if tg < 6 else 1):
                self.adaln_block(0, self.ada_next)
                self.ada_next += 1

    @staticmethod
    def blk_of_tile(tg):
        return 0 if tg < 2 else 1 + (tg - 2) // 4

    def adaln_block(self, i, og):
        nc, S, I = self.nc, self.S, self.I
        p6 = self.P[6]
        WA = self.scr(18432, 4096).rearrange("p (b k n) -> p b k n", b=2, k=KC)
        ROW2 = self.scr(22528, 1024, F32).rearrange("p (b n) -> p b n", b=2)
        MODN = self.MODS[i % 2]
        mk = ("MOD", i % 2, og // 4)
        self.adaln_flush()
        b = og % 2
        ROW = ROW2[0:2, b, :]
        S.dma_split("pool", WA[:, b], I["w_ada"][i, :, og * 256:(og + 1) * 256].rearrange("(k p) n -> p k n", p=128), ("WA", b))
        for kc in range(KC):
            S.op("pe", lambda: nc.tensor.matmul(p6[0:2, 0:256], self.condTb[:, kc, :], WA[:, b, kc, :], start=(kc == 0), stop=(kc == KC - 1)),
                 reads=[("WA", b), "condTb"], writes=[("P", 6)])
        S.op("act", lambda: nc.scalar.activation(out=ROW, in_=p6[0:2, 0:256], func=AF.Copy), reads=[("P", 6)], writes=[("ROW", b)])

        def part_b():
            for o2 in range(2):
                S.op("pe", lambda: nc.tensor.transpose(p6[:, 256 + o2 * 2:258 + o2 * 2], ROW[:, o2 * 128:(o2 + 1) * 128], self.identf[0:2, 0:2]),
                     reads=[("ROW", b), "identf"], writes=[("P", 6)])
            modv = MODN[:].rearrange("p s c l -> p (s c) l")[:, og * 2:og * 2 + 2, :]
            bad = self.BADA[:, i * 48 + og * 2:i * 48 + og * 2 + 2].unsqueeze(2).to_broadcast([128, 2, 2])
            S.op("dve", lambda: nc.vector.tensor_tensor(out=modv, in0=p6[:, 256:260].rearrange("p (o l) -> p o l", l=2), in1=bad, op=ALU.add),
                 reads=[("P", 6), "BADA0", "BADA1"], writes=[mk])

        self.adaln_pending = part_b

    def adaln_flush(self):
        if getattr(self, "adaln_pending", None) is not None:
            f = self.adaln_pending
            self.adaln_pending = None
            f()

    def adaln_finish(self, i, which):
        nc, S = self.nc, self.S
        self.adaln_flush()
        MODN = self.MODS[i % 2]
        s, gsel = (1, 0) if which == 1 else (4, 1)
        mk = ("MOD", i % 2, s)
        g = self.NG[:, (i * 2 + gsel) * KC:(i * 2 + gsel + 1) * KC].unsqueeze(2).to_broadcast([128, KC, 2])
        S.op("dve", lambda: nc.vector.scalar_tensor_tensor(out=MODN[:, s], in0=MODN[:, s], scalar=1.0, in1=g,
                                                           op0=ALU.add, op1=ALU.mult),
             reads=[mk, "NG"], writes=[mk])

    def norm_mod_all(self, blks, which, dst, after_block=None):
        for _ in self.norm_mod_gen(blks, which, dst, after_block):
            pass

    def norm_mod_gen(self, blks, which, dst, after_block=None, MOD=None, modkey=None):
        nc, S = self.nc, self.S
        sA, sB = (1, 0) if which == 1 else (4, 3)
        pm = self.P[6]
        MOD = self.MOD if MOD is None else MOD
        modkey = self.modkey if modkey is None else modkey

        def stage_a(blk):
            t0, n = BLKS[blk]
            S.op("act", lambda: nc.scalar.activation(out=self.SQ[:, 0:4, 0:n], in_=self.XT[:, 0:4, t0:t0 + n], func=AF.Square),
                 reads=[("XT", c, blk) for c in range(4)], writes=[("SQ", 0)])
            S.op("dve", lambda: nc.vector.tensor_tensor(out=self.SQ[:, 4:8, 0:n], in0=self.XT[:, 4:8, t0:t0 + n], in1=self.XT[:, 4:8, t0:t0 + n], op=ALU.mult),
                 reads=[("XT", c, blk) for c in range(4, 8)], writes=[("SQ", 1)])
            for c in range(KC):
                S.op("pe", lambda: nc.tensor.matmul(pm[:, 0:n], self.onesb[:], self.SQ[:, c, 0:n], start=(c == 0), stop=(c == KC - 1)),
                     reads=[("SQ", c // 4), "onesb"], writes=[("P", 6)])

        def stage_b(blk):
            t0, n = BLKS[blk]
            rs = self.RS[:, 0, 0:n]
            S.op("act", lambda: nc.scalar.activation(out=rs, in_=pm[:, 0:n], func=AF.Ln, bias=self.epsT[:, 0:1]),
                 reads=[("P", 6), "epsT"], writes=["RS"])
            S.op("act", lambda: nc.scalar.activation(out=rs, in_=rs, func=AF.Exp, scale=-0.5), reads=["RS"], writes=["RS"])

        def stage_c(blk):
            t0, n = BLKS[blk]
            col = 1 if blk == 0 else 0
            rs = self.RS[:, 0, 0:n]
            for c in range(KC):
                ti = self.tmpf_i
                self.tmpf_i ^= 1
                tmp = self.TMPF[:, ti, 0:n]
                S.op("dve", lambda: nc.vector.tensor_tensor(out=tmp, in0=self.XT[:, c, t0:t0 + n], in1=rs, op=ALU.mult),
                     reads=[("XT", c, blk), "RS"], writes=[("TMPF", ti)])
                d_ap, d_key = dst(blk, c)
                S.op("act", lambda: nc.scalar.activation(out=d_ap, in_=tmp, func=AF.Identity,
                                                          scale=MOD[:, sA, c, col:col + 1], bias=MOD[:, sB, c, col:col + 1]),
                     reads=[("TMPF", ti), modkey + (sA,), modkey + (sB,)], writes=[d_key])
            if after_block is not None:
                after_block(blk)

        prev = None
        for blk in blks:
            stage_a(blk)
            if prev is not None:
                stage_c(prev)
            stage_b(blk)
            prev = blk
            yield
        stage_c(prev)

    def emit_fnet(self, i):
        nc, S, I = self.nc, self.S, self.I
        HB = self.scr(0, 4096).rearrange("p (c t) -> p c t", c=KC)
        CS = self.scr(4096, 256)
        WFO = self.scr(4352, 8192).rearrange("p (k n) -> p k n", k=KC)
        DF = self.scr(12544, 2 * 4096).rearrange("p (b z l q) -> p b z l q", b=2, z=2, l=16)
        YTOK = self.scr(20736, 1024)
        FBO = self.scr(21760, 16, F32)
        ZT = self.ARENA[:, :].rearrange("p (t z n) -> p t z n", t=NT, z=2)
        YT = HB

        S.dma("sp", CS, I["cs128"], writes=["CS"])
        S.dma_split("pool", WFO, I["fnet_w_out"].rearrange("(k p) n -> p k n", p=128), "WFO")
        self.load_fm(FBO, I["fnet_b_out"], 8, "FBO")

        def z_block(blk):
            t0, n = BLKS[blk]
            for tl in range(n // 128):
                tg = (t0 // 128) + tl
                for gp in range(4):
                    pz = self.P[gp % 2]
                    pk = ("P", gp % 2)
                    for gg in range(2):
                        g = gp * 2 + gg
                        S.op("pe", lambda: nc.tensor.matmul(pz[:, gg * 256:(gg + 1) * 256], HB[:, g, tl * 128:(tl + 1) * 128], CS,
                                                            start=True, stop=True),
                             reads=[("HB", g), "CS"], writes=[pk])
                    src = pz[:].rearrange("p (g z c) -> p z g c", g=2, z=2)
                    dstz = ZT[:, tg, :, gp * 256:(gp + 1) * 256].rearrange("p z (g c) -> p z g c", g=2)
                    if gp % 2 == 0:
                        S.op("dve", lambda: nc.vector.tensor_copy(out=dstz, in_=src), reads=[pk], writes=[("ZT", tg)])
                    else:
                        S.op("act", lambda: nc.scalar.activation(out=dstz, in_=src, func=AF.Copy), reads=[pk], writes=[("ZT", tg)])

        self.norm_mod_all(list(range(5)), 1, lambda blk, c: (HB[:, c, 0:BLKS[blk][1]], ("HB", c)), after_block=z_block)

        dcount = 0
        for blk in range(5):
            t0, n = BLKS[blk]
            col = 1 if blk == 0 else 0
            if blk == 0:
                nlc, tbase, dsrc = 2, 0, I["dft_ctx"]
            else:
                nlc, tbase, dsrc = 16, 2, I["dft_lat"]
            for tl in range(n // 128):
                j = (t0 // 128 - tbase) + tl
                b = dcount % 2
                dcount += 1
                S.dma("sp", DF[:, b, :, 0:nlc, :], dsrc[j], writes=[("DF", b)])
                for half in range(2):
                    py = self.P[2 + half]
                    pk = ("P", 2 + half)
                    nmm = 2 * nlc
                    k = 0
                    for z in range(2):
                        for lc in range(nlc):
                            S.op("pe", lambda: nc.tensor.matmul(py[:], DF[:, b, z, lc, :], ZT[:, tbase + lc, z, half * 512:(half + 1) * 512],
                                                                start=(k == 0), stop=(k == nmm - 1)),
                                 reads=[("DF", b), ("ZT", tbase + lc)], writes=[pk])
                            k += 1
                    if half == 0:
                        S.op("dve", lambda: nc.vector.tensor_copy(out=YTOK[:, 0:512], in_=py[:]), reads=[pk], writes=[("YTOK", 0)])
                    else:
                        S.op("act", lambda: nc.scalar.activation(out=YTOK[:, 512:1024], in_=py[:], func=AF.Copy), reads=[pk],
                             writes=[("YTOK", 1)])
                ptb = self.PTB
                for c in range(KC):
                    S.op("pe", lambda: nc.tensor.transpose(ptb[:, c * 128:(c + 1) * 128], YTOK[:, c * 128:(c + 1) * 128], self.identb[:]),
                         reads=[("YTOK", c // 4), "identb"], writes=[("P", 7)])
                S.op("dve", lambda: nc.vector.tensor_copy(out=YT[:, :, tl * 128:(tl + 1) * 128],
                                                          in_=ptb[:].rearrange("p (c t) -> p c t", c=KC)),
                     reads=[("P", 7)], writes=[("HB", c) for c in range(KC)])
            for oc in range(KC):
                po = self.P[5]
                for kc in range(KC):
                    S.op("pe", lambda: nc.tensor.matmul(po[:, 0:n], WFO[:, kc, oc * 128:(oc + 1) * 128], YT[:, kc, 0:n],
                                                        start=(kc == 0), stop=(kc == KC - 1)),
                         reads=["WFO", ("HB", kc)], writes=[("P", 5)])
                ti = self.tmpf_i
                self.tmpf_i ^= 1
                tmp = self.TMPF[:, ti, 0:n]
                S.op("act", lambda: nc.scalar.activation(out=tmp, in_=po[:, 0:n], func=AF.Identity, bias=FBO[:, oc:oc + 1]),
                     reads=[("P", 5), "FBO"], writes=[("TMPF", ti)])
                xs = self.XT[:, oc, t0:t0 + n]
                S.op("dve", lambda: nc.vector.scalar_tensor_tensor(out=xs, in0=tmp, scalar=self.MOD[:, 2, oc, col:col + 1], in1=xs,
                                                                   op0=ALU.mult, op1=ALU.add),
                     reads=[("TMPF", ti), self.modkey + (2,), ("XT", oc, blk)], writes=[("XT", oc, blk)])

    def emit_attn(self, i, kind):
        nc, S, I = self.nc, self.S, self.I
        need_ctx = i < DEPTH - 1
        diff = kind == "diff"
        hd = 64 if diff else 128
        scale = float(hd) ** -0.5
        w_in = I["diff_w_in"] if diff else I["gqa_w_in"]
        w_out = I["diff_w_out"] if diff else I["gqa_w_out"]
        ngroups = 8 if diff else 2
        nstreams = 2 if diff else 4
        nqw = 1 if diff else 4
        off = [0]

        def take(n):
            a = off[0]
            off[0] += n
            return a

        ROPE = self.scr(take(8192), 8192, F32).rearrange("p (z t) -> p z t", z=2)
        WQ = self.scr(take(4096), 4096).rearrange("p (k n) -> p k n", k=KC)
        WK = self.scr(take(1024), 1024).rearrange("p (k n) -> p k n", k=KC)
        WV = self.scr(take(1024), 1024).rearrange("p (k n) -> p k n", k=KC)
        WOH2 = self.scr(take(2048), 2048).rearrange("p (b n) -> p b n", b=2)
        QF = self.scr(take(1024), 1024, F32)
        RSQ = self.scr(take(1024), 1024, F32)
        T1 = self.scr(take(1024), 1024, F32)
        T2 = self.scr(take(1024), 1024, F32)
        PT = self.scr(take(1024), 1024).rearrange("p (b n) -> p b n", b=2)
        O1N = self.scr(take(1024), 1024, F32).rearrange("p (q e) -> p q e", q=4)
        WW = QF
        OB = self.SCR[:, 0:0]
        MISC = self.scr(take(64), 64, F32)
        SGB = self.scr(take(256), 256, F32)
        BO = self.scr(take(128), 128)
        PERM = self.scr(take(256), 256, F32)
        assert off[0] <= 23552
        OB = RSQ.bitcast(BF16)[:, 0:512]
        QT = [self.big(s_ * T, T) for s_ in range(nstreams)]
        KT = self.big(4 * T, T)
        V = self.big(5 * T, NT * 132).rearrange("p (t e) -> p t e", t=NT)
        OT = self.big(5 * T + NT * 132, T)
        mul, add = ALU.mult, ALU.add

        S.dma("sp", ROPE, I["rope64" if diff else "rope128"], writes=["ROPE"])
        S.dma("sp", BO, I["bo64" if diff else "bo128"], writes=["BO"])
        S.dma("sp", PERM, I["perm64" if diff else "perm128"], writes=["PERM"])
        self.load_fm(MISC[:, 0:4], I["diff_g" if diff else "gqa_g"], 4, "GAINS")
        S.dma("sp", MISC[:, 4:6], I["cmask"], writes=["CMASK"])
        S.op("pool", lambda: nc.gpsimd.memset(V[:, :, 128:129], 1.0), writes=["Vones"])
        S.op("pool", lambda: nc.gpsimd.memset(MISC[:, 24:25], -20.0), writes=["NEGC"])
        if diff:
            lam_init = 0.8 - 0.6 * math.exp(-0.3 * i)
            LPB = T1[:, 0:256]
            S.dma("sp", LPB, I["diff_lambda"].partition_broadcast(128), writes=["T1"])
            S.dma("sp", SGB, I["diff_subln"].partition_broadcast(128), writes=["SGB"])
            tmp = T2[:, 0:128]
            S.op("dve", lambda: nc.vector.tensor_tensor(out=tmp[:, 0:64], in0=LPB[:, 0:64], in1=LPB[:, 64:128], op=mul),
                 reads=["T1"], writes=["T2"])
            S.op("dve", lambda: nc.vector.tensor_tensor(out=tmp[:, 64:128], in0=LPB[:, 128:192], in1=LPB[:, 192:256], op=mul),
                 reads=["T1", "T2"], writes=["T2"])
            S.op("dve", lambda: nc.vector.reduce_sum(out=MISC[:, 11:12], in_=tmp[:, 0:64], axis=AX.X), reads=["T2"], writes=["LAM"])
            S.op("dve", lambda: nc.vector.reduce_sum(out=MISC[:, 12:13], in_=tmp[:, 64:128], axis=AX.X), reads=["T2", "LAM"], writes=["LAM"])
            S.op("act", lambda: nc.scalar.activation(out=MISC[:, 11:13], in_=MISC[:, 11:13], func=AF.Exp), reads=["LAM"], writes=["LAM"])
            S.op("dve", lambda: nc.vector.scalar_tensor_tensor(out=MISC[:, 6:7], in0=MISC[:, 12:13], scalar=-lam_init, in1=MISC[:, 11:12],
                                                               op0=add, op1=ALU.subtract), reads=["LAM"], writes=["NLAM"])
            S.op("dve", lambda: nc.vector.tensor_scalar_mul(out=SGB, in0=SGB, scalar1=1.0 - lam_init), reads=["SGB"], writes=["SGB"])

        if not self.norm_done:
            self.norm_mod_all(list(range(5)), 1, lambda blk, c: (self.HT[:, c, BLKS[blk][0]:BLKS[blk][0] + BLKS[blk][1]], ("HT", c, blk)))
        self.norm_done = False

        p4, p5, p6 = self.P[4], self.P[5], self.P[6]
        S.barrier()
        SQF = self.SQ[:].rearrange("p c n -> p (c n)")
        BSETS = [
            dict(proj=self.P[4], projk=("P", 4), pss=self.P[5], pssk=("P", 5), prm=self.P[6], prmk=("P", 6),
                 QF=QF, QFk="QF", sq=SQF[:, 0:512], sqk="SQa", RSQ=RSQ, RSQk="RSQ", T1=T1, T1k="T1", T2=T2, T2k="T2"),
            dict(proj=self.P[0], projk=("P", 0), pss=self.P[1], pssk=("P", 1), prm=self.P[2], prmk=("P", 2),
                 QF=SQF[:, 1024:2048].bitcast(F32), QFk="QF1", sq=SQF[:, 512:1024], sqk="SQb",
                 RSQ=SQF[:, 2048:3072].bitcast(F32), RSQk="RSQ1", T1=SQF[:, 3072:4096].bitcast(F32), T1k="T1b",
                 T2=self.TMPF[:, 0, :], T2k="T2b"),
        ]

        def qk_chain(kind, blk, sq_, st):
            B = BSETS[st]
            t0, n = BLKS[blk]
            gcol = 2 if kind == "k" else 0
            Wm = WK if kind == "k" else WQ[:, :, sq_ * 128:(sq_ + 1) * 128]
            wkey = "WK" if kind == "k" else "WQ"
            for kc in range(KC):
                S.op("pe", lambda: nc.tensor.matmul(B["proj"][:, 0:n], Wm[:, kc, :], self.HT[:, kc, t0:t0 + n], start=(kc == 0), stop=(kc == KC - 1)),
                     reads=[wkey, ("HT", kc, blk)], writes=[B["projk"]])
            yield
            S.op("act", lambda: nc.scalar.activation(out=B["QF"][:, 0:n], in_=B["proj"][:, 0:n], func=AF.Copy), reads=[B["projk"]], writes=[B["QFk"]])
            S.op("act", lambda: nc.scalar.activation(out=B["sq"][:, 0:n], in_=B["proj"][:, 0:n], func=AF.Square), reads=[B["projk"]], writes=[B["sqk"]])
            yield
            S.op("pe", lambda: nc.tensor.matmul(B["pss"][:, 0:n], BO, B["sq"][:, 0:n], start=True, stop=True), reads=["BO", B["sqk"]], writes=[B["pssk"]])
            if blk != 0:
                S.op("pe", lambda: nc.tensor.matmul(B["prm"][:, 0:n], PERM, B["QF"][:, 0:n], start=True, stop=True), reads=["PERM", B["QFk"]],
                     writes=[B["prmk"]])
            yield
            S.op("act", lambda: nc.scalar.activation(out=B["RSQ"][:, 0:n], in_=B["pss"][:, 0:n], func=AF.Ln, bias=self.epsT[:, 0:1]),
                 reads=[B["pssk"], "epsT"], writes=[B["RSQk"]])
            S.op("act", lambda: nc.scalar.activation(out=B["RSQ"][:, 0:n], in_=B["RSQ"][:, 0:n], func=AF.Exp, scale=-0.5), reads=[B["RSQk"]],
                 writes=[B["RSQk"]])
            yield
            t1 = B["T1"][:, 0:n]
            if blk == 0:
                S.op("dve", lambda: nc.vector.scalar_tensor_tensor(out=t1, in0=B["QF"][:, 0:n], scalar=MISC[:, gcol:gcol + 1], in1=B["RSQ"][:, 0:n],
                                                                   op0=mul, op1=mul), reads=[B["QFk"], B["RSQk"], "GAINS"], writes=[B["T1k"]])
            else:
                l0 = t0 - CTX
                t2 = B["T2"][:, 0:n]
                S.op("dve", lambda: nc.vector.scalar_tensor_tensor(out=t1, in0=B["QF"][:, 0:n], scalar=MISC[:, gcol:gcol + 1],
                                                                   in1=ROPE[:, 0, l0:l0 + n], op0=mul, op1=mul),
                     reads=[B["QFk"], "GAINS", "ROPE"], writes=[B["T1k"]])
                S.op("dve", lambda: nc.vector.scalar_tensor_tensor(out=t2, in0=B["prm"][:, 0:n], scalar=MISC[:, gcol + 1:gcol + 2],
                                                                   in1=ROPE[:, 1, l0:l0 + n], op0=mul, op1=mul),
                     reads=[B["prmk"], "GAINS", "ROPE"], writes=[B["T2k"]])
                yield
                S.op("dve", lambda: nc.vector.tensor_tensor(out=t1, in0=t1, in1=t2, op=add), reads=[B["T1k"], B["T2k"]], writes=[B["T1k"]])
                S.op("dve", lambda: nc.vector.tensor_tensor(out=t1, in0=t1, in1=B["RSQ"][:, 0:n], op=mul), reads=[B["T1k"], B["RSQk"]], writes=[B["T1k"]])
            yield
            if kind == "k":
                S.op("act", lambda: nc.scalar.activation(out=KT[:, t0:t0 + n], in_=t1, func=AF.Copy), reads=[B["T1k"]], writes=[("KT", blk)])
            elif diff:
                S.op("dve", lambda: nc.vector.tensor_scalar_mul(out=QT[0][:, t0:t0 + n], in0=t1, scalar1=MISC[:, 4:5]),
                     reads=[B["T1k"], "CMASK"], writes=[("QT", 0, blk)])
                S.op("act", lambda: nc.scalar.activation(out=QT[1][:, t0:t0 + n], in_=t1, func=AF.Copy, scale=MISC[:, 5:6]),
                     reads=[B["T1k"], "CMASK"], writes=[("QT", 1, blk)])
            else:
                S.op("act", lambda: nc.scalar.activation(out=QT[sq_][:, t0:t0 + n], in_=t1, func=AF.Copy), reads=[B["T1k"]],
                     writes=[("QT", sq_, blk)])

        def run_chains(items):
            for a in range(0, len(items), 2):
                gens = [qk_chain(*it, st) for st, it in enumerate(items[a:a + 2])]
                live = list(gens)
                while live:
                    nxt = []
                    for gq in live:
                        try:
                            next(gq)
                            nxt.append(gq)
                        except StopIteration:
                            pass
                    live = nxt

        qblks = ([0] if need_ctx else []) + [1, 2, 3, 4]
        pend = []
        for g in range(ngroups):
            if diff:
                q0, qn, k0, v0 = g * 128, 128, 1024 + g * 128, 2048 + g * 128
            else:
                q0, qn, k0, v0 = g * 512, 512, 1024 + g * 128, 1280 + g * 128
            S.dma_split("pool", WQ[:, :, 0:qn], w_in[:, q0:q0 + qn].rearrange("(k p) n -> p k n", p=128), "WQ")
            S.dma_split("pool", WK, w_in[:, k0:k0 + 128].rearrange("(k p) n -> p k n", p=128), "WK")
            S.dma_split("pool", WV, w_in[:, v0:v0 + 128].rearrange("(k p) n -> p k n", p=128), "WV")
            for v4 in range((NT + 3) // 4):
                cnt = min(4, NT - v4 * 4)
                bi = 4 if v4 % 2 == 0 else 0
                pb, pbk = self.P[bi], ("P", bi)
                for jj in range(cnt):
                    tg = v4 * 4 + jj
                    blk = self.blk_of_tile(tg)
                    for kc in range(KC):
                        S.op("pe", lambda: nc.tensor.matmul(pb[:, jj * 128:(jj + 1) * 128], self.HT[:, kc, tg * 128:(tg + 1) * 128], WV[:, kc, :],
                                                            start=(kc == 0), stop=(kc == KC - 1)),
                             reads=["WV", ("HT", kc, blk)], writes=[pbk])
                src = pb[:, 0:cnt * 128].rearrange("p (j e) -> p j e", e=128)
                vk = [("V", v4 * 4 + jj) for jj in range(cnt)]
                if v4 % 2 == 0:
                    S.op("dve", lambda: nc.vector.tensor_copy(out=V[:, v4 * 4:v4 * 4 + cnt, 0:128], in_=src), reads=[pbk], writes=vk)
                else:
                    S.op("act", lambda: nc.scalar.activation(out=V[:, v4 * 4:v4 * 4 + cnt, 0:128], in_=src, func=AF.Copy), reads=[pbk], writes=vk)
            while pend:
                pend.pop(0)()
            items = [("k", blk, 0) for blk in range(5)] + [("q", blk, sq_) for sq_ in range(nqw) for blk in qblks]
            run_chains(items)
            sets = [[0, 1]] if diff else [[0], [1], [2], [3]]
            PO = [(p6, ("P", 6)), (self.PTB[:].bitcast(F32), ("P", 7))]
            PTS = [(PT[:, 0, :], ("PT", 0)), (PT[:, 1, :], ("PT", 1)), (T1.bitcast(BF16)[:, 0:512], "T1")]
            for sset in sets:
                head = g if diff else g * 4 + sset[0]
                wi = self.woh_cnt % 2
                self.woh_cnt += 1
                WOHb = WOH2[:, wi, :]
                S.dma("pool", WOHb, w_out[head * 128:(head + 1) * 128, :], writes=[("WOH", wi)])
                for blk in qblks:
                    t0, n = BLKS[blk]
                    col = 1 if blk == 0 else 0
                    nq = n // 128
                    kts = [0, 1] if blk == 0 else list(range(NT))
                    nk = len(kts)
                    for s_ in sset:
                        aset = self.acc_unit % 2
                        self.acc_unit += 1
                        ACC = self.PACC[aset]
                        akeys = [("P", 2 + 2 * aset), ("P", 3 + 2 * aset)][0:(nq + 1) // 2]
                        for step in range(nk + 2):
                            if step < nk:
                                ki, kt = step, kts[step]
                                ps = self.P[ki % 2]
                                ptb_, ptk = PTS[ki % 3]
                                S.op("pe", lambda: nc.tensor.matmul(ps[:, 0:n], KT[:, kt * 128:(kt + 1) * 128], QT[s_][:, t0:t0 + n], start=True, stop=True),
                                     reads=[("KT", self.blk_of_tile(kt)), ("QT", s_, blk)], writes=[("P", ki % 2)])
                                S.op("act", lambda: nc.scalar.activation(out=ptb_[:, 0:n], in_=ps[:, 0:n], func=AF.Exp, scale=scale, bias=MISC[:, 24:25]),
                                     reads=[("P", ki % 2), "NEGC"], writes=[ptk])
                            if step >= 2:
                                ki, kt = step - 2, kts[step - 2]
                                ptb_, ptk = PTS[ki % 3]
                                for qt in range(nq):
                                    acc = ACC[:, qt * 256:qt * 256 + 129]
                                    S.op("pe", lambda: nc.tensor.matmul(acc, ptb_[:, qt * 128:(qt + 1) * 128], V[:, kt, 0:129],
                                                                        start=(ki == 0 and qt % 2 == 0), stop=(ki == nk - 1),
                                                                        skip_group_check=True),
                                         reads=[ptk, ("V", kt), "Vones"], writes=[("P", 2 + 2 * aset + qt // 2)])
                            if step >= 7 and pend:
                                pend.pop(0)()
                        if s_ == sset[-1]:
                            while pend:
                                pend.pop(0)()
                        A4 = ACC.rearrange("p (q c) -> p q c", c=256)[:, 0:nq, :]
                        R = MISC[:, 16:16 + nq]
                        S.op("dve", lambda: nc.vector.reciprocal(out=R.unsqueeze(2), in_=A4[:, :, 128:129]), reads=akeys, writes=["R"])
                        have_ob = False
                        if diff and s_ == 0:
                            S.op("dve", lambda: nc.vector.tensor_tensor(out=O1N[:, 0:nq, :], in0=A4[:, :, 0:128],
                                                                        in1=R.unsqueeze(2).to_broadcast([128, nq, 128]), op=mul),
                                 reads=akeys + ["R"], writes=["O1N"])
                        elif diff:
                            W3 = WW[:, 0:nq * 128].rearrange("p (q e) -> p q e", e=128)
                            T3 = T2[:, 0:nq * 128].rearrange("p (q e) -> p q e", e=128)
                            O3 = OB[:, 0:nq * 128].rearrange("p (q e) -> p q e", e=128)
                            SSQ = MISC[:, 20:20 + nq]
                            S.op("dve", lambda: nc.vector.tensor_scalar_mul(out=R, in0=R, scalar1=MISC[:, 6:7]), reads=["R", "NLAM"], writes=["R"])
                            S.op("dve", lambda: nc.vector.tensor_tensor(out=W3, in0=A4[:, :, 0:128], in1=R.unsqueeze(2).to_broadcast([128, nq, 128]), op=mul),
                                 reads=akeys + ["R"], writes=["QF"])
                            S.op("dve", lambda: nc.vector.tensor_tensor(out=W3, in0=W3, in1=O1N[:, 0:nq, :], op=add), reads=["QF", "O1N"], writes=["QF"])
                            S.op("dve", lambda: nc.vector.tensor_tensor(out=T3, in0=W3, in1=W3, op=mul), reads=["QF"], writes=["T2"])
                            S.op("dve", lambda: nc.vector.reduce_sum(out=SSQ, in_=T3, axis=AX.X), reads=["T2"], writes=["SSQ"])
                            S.op("act", lambda: nc.scalar.activation(out=SSQ, in_=SSQ, func=AF.Ln, scale=1.0 / 128, bias=self.epsT[:, 0:1]),
                                 reads=["SSQ", "epsT"], writes=["SSQ"])
                            S.op("act", lambda: nc.scalar.activation(out=SSQ, in_=SSQ, func=AF.Exp, scale=-0.5), reads=["SSQ"], writes=["SSQ"])
                            S.op("dve", lambda: nc.vector.tensor_tensor(out=W3, in0=W3, in1=SSQ.unsqueeze(2).to_broadcast([128, nq, 128]), op=mul),
                                 reads=["QF", "SSQ"], writes=["QF"])
                            S.op("dve", lambda: nc.vector.tensor_tensor(out=O3, in0=W3, in1=SGB.unsqueeze(1).to_broadcast([128, nq, 128]), op=mul),
                                 reads=["QF", "SGB"], writes=["RSQ"])
                            have_ob = True
                        else:
                            O3 = OB[:, 0:nq * 128].rearrange("p (q e) -> p q e", e=128)
                            S.op("dve", lambda: nc.vector.tensor_tensor(out=O3, in0=A4[:, :, 0:128], in1=R.unsqueeze(2).to_broadcast([128, nq, 128]), op=mul),
                                 reads=akeys + ["R"], writes=["RSQ"])
                            have_ob = True
                        if have_ob:
                            def mk_tr(t0=t0, n=n, nq=nq, blk=blk):
                                def f():
                                    for qt in range(nq):
                                        S.op("pe", lambda: nc.tensor.transpose(self.PTB[:, qt * 128:(qt + 1) * 128], OB[:, qt * 128:(qt + 1) * 128], self.identb[:]),
                                             reads=["RSQ", "identb"], writes=[("P", 7)])
                                    S.op("dve", lambda: nc.vector.tensor_copy(out=OT[:, t0:t0 + n], in_=self.PTB[:, 0:n]), reads=[("P", 7)], writes=[("OT", blk)])
                                return f

                            def mk_op(oc, t0=t0, n=n, blk=blk, col=col, wi=wi, WOHb=WOHb):
                                def f():
                                    pp, ppk = PO[oc % 2]
                                    S.op("pe", lambda: nc.tensor.matmul(pp[:, 0:n], WOHb[:, oc * 128:(oc + 1) * 128], OT[:, t0:t0 + n], start=True, stop=True),
                                         reads=[("WOH", wi), ("OT", blk)], writes=[ppk])
                                    xs = self.XT[:, oc, t0:t0 + n]
                                    S.op("dve", lambda: nc.vector.scalar_tensor_tensor(out=xs, in0=pp[:, 0:n], scalar=self.MOD[:, 2, oc, col:col + 1], in1=xs,
                                                                                       op0=mul, op1=add),
                                         reads=[ppk, self.modkey + (2,), ("XT", oc, blk)], writes=[("XT", oc, blk)])
                                return f

                            pend = [mk_tr()] + [mk_op(oc) for oc in range(KC)]
        while pend:
            pend.pop(0)()

    def emit_hgrn(self, i):
        nc, S, I = self.nc, self.S, self.I
        w_in, w_out = I["hgrn_w_in"], I["hgrn_w_out"]
        mul, add, sub = ALU.mult, ALU.add, ALU.subtract
        NCH = T // 64
        BT = []
        for o_ in (0, 3584):
            BT.append(dict(SIG=self.scr(o_, 1024, F32), PRE=self.scr(o_ + 1024, 1024, F32), EX=self.scr(o_ + 2048, 1024, F32),
                           KKb=self.scr(o_ + 3072, 512)))
        QTLd = [self.scr(7168, 2304), self.scr(11776, 2304)]
        KTLd = [self.scr(9472, 2304), self.scr(14080, 2304)]
        KTOKG = self.scr(16384, 2048)[0:64].rearrange("p (b j k) -> p b j k", b=2, j=8)
        EPN = self.scr(18432, 4608).rearrange("p (n v) -> p n v", n=NCH)
        S32P = self.scr(23040, 512, F32).rearrange("p (b v) -> p b v", b=2)
        SQO = self.scr(4608, 9216, F32)[0:64].rearrange("p (j v) -> p j v", j=NCH)
        GS = self.scr(0, 4608)[0:64].rearrange("p (j v) -> p j v", j=NCH)
        OT = self.scr(13824, 2304)
        OACC = self.big(0, 9216, F32)[0:64].rearrange("p (j v) -> p j v", j=NCH)
        V = self.big(9216, 4608)[0:64].rearrange("p (j v) -> p j v", j=NCH)
        KKF = self.ARENA[:, self.BIGOFF + 13824:self.BIGOFF + 18432]
        QS = KKF[:, 0:2304]
        SCG = KKF[0:64, 2304:3328].rearrange("p (b j t) -> p b j t", b=2, j=8)
        OB = GS
        SQF = self.SQ[:].rearrange("p c n -> p (c n)")
        W = SQF[:, 0:2048].rearrange("p (b k n) -> p b k n", b=2, k=KC)
        WOH = SQF[:, 2048:3072]
        S32 = SQF[:, 3072:3328].bitcast(F32)
        MSK = SQF[0:64, 3328:3584].bitcast(F32).rearrange("p (d t) -> p d t", d=2)
        GB = SQF[0:64, 3584:3840].bitcast(F32)
        LBR = SQF[:, 3840:3968].bitcast(F32)
        TF = self.TMPF[:].rearrange("p a n -> p (a n)")
        LBS = TF[:, 0:16].rearrange("p (d c) -> p d c", d=2)
        LBV = TF[:, 16:32].rearrange("p (d c) -> p d c", d=2)
        OML = TF[:, 32:48].rearrange("p (d c) -> p d c", d=2)
        NOML = TF[:, 48:64].rearrange("p (d c) -> p d c", d=2)
        EBLd = [TF[:, 64:64 + NCH], TF[:, 320:320 + NCH]]
        CM512 = TF[:, 512:768].bitcast(BF16)
        SSQ = TF[0:64, 128:128 + NCH]
        RSTD = TF[0:64, 192:192 + NCH]
        EBLNB = TF[:, 256:256 + NCH]

        if not self.norm_done:
            self.norm_mod_all(list(range(5)), 1, lambda blk, c: (self.HT[:, c, BLKS[blk][0]:BLKS[blk][0] + BLKS[blk][1]], ("HT", c, blk)))
        self.norm_done = False
        S.barrier()

        S.dma("sp", CM512, I["cmscan"][:, 0:512], writes=["CM"])
        S.dma("sp", MSK, I["hmask"], writes=["MSK"])
        GCOL = TF[:, 768:769]
        ONEC = TF[:, 769:770]
        S.op("dve", lambda: nc.vector.memset(ONEC, 1.0), writes=["ONEC"])
        P80, N80 = TF[:, 770:771], TF[:, 771:772]
        S.op("dve", lambda: nc.vector.memset(P80, 80.0), writes=["C80"])
        S.op("dve", lambda: nc.vector.memset(N80, -80.0), reads=["C80"], writes=["C80"])
        self.load_fm(GCOL, I["hgrn_gain"].rearrange("(o v) -> o v", o=1), 1, "GCOL")
        self.load_fm(LBR, I["hgrn_lb"], 64, "LBR")
        S.op("act", lambda: nc.scalar.activation(out=LBR, in_=LBR, func=AF.Exp), reads=["LBR"], writes=["LBR"])
        E4 = LBR.rearrange("p (d l c) -> p d l c", d=2, l=4)
        S.op("dve", lambda: nc.vector.tensor_tensor(out=LBS, in0=E4[:, :, 0, :], in1=E4[:, :, 1, :], op=add), reads=["LBR"], writes=["LBS"])
        S.op("dve", lambda: nc.vector.tensor_tensor(out=LBS, in0=LBS, in1=E4[:, :, 2, :], op=add), reads=["LBR", "LBS"], writes=["LBS"])
        S.op("dve", lambda: nc.vector.tensor_tensor(out=LBS, in0=LBS, in1=E4[:, :, 3, :], op=add), reads=["LBR", "LBS"], writes=["LBS"])
        S.op("dve", lambda: nc.vector.reciprocal(out=LBS, in_=LBS), reads=["LBS"], writes=["LBS"])
        S.op("dve", lambda: nc.vector.tensor_copy(out=LBV, in_=E4[:, :, 1, :]), reads=["LBR"], writes=["LBV"])
        for l in range(2, i + 1):
            S.op("dve", lambda: nc.vector.tensor_tensor(out=LBV, in0=LBV, in1=E4[:, :, l, :], op=add), reads=["LBR", "LBV"], writes=["LBV"])
        S.op("dve", lambda: nc.vector.tensor_tensor(out=LBV, in0=LBV, in1=LBS, op=mul), reads=["LBV", "LBS"], writes=["LBV"])
        S.op("dve", lambda: nc.vector.tensor_scalar(out=OML, in0=LBV, scalar1=-1.0, scalar2=1.0, op0=mul, op1=add), reads=["LBV"], writes=["OML"])
        S.op("dve", lambda: nc.vector.tensor_scalar_add(out=NOML, in0=LBV, scalar1=-1.0), reads=["LBV"], writes=["NOML"])

        p6 = self.P[6]
        wcnt = [0]

        def load_w(col0):
            b = wcnt[0] % 2
            wcnt[0] += 1
            S.dma_split("pool", W[:, b], w_in[:, col0:col0 + 128].rearrange("(k p) n -> p k n", p=128), ("W", b))
            return b

        def proj_fm(b, blk):
            t0, n = BLKS[blk]
            for kc in range(KC):
                S.op("pe", lambda: nc.tensor.matmul(p6[:, 0:n], W[:, b, kc, :], self.HT[:, kc, t0:t0 + n], start=(kc == 0), stop=(kc == KC - 1)),
                     reads=[("W", b), ("HT", kc, blk)], writes=[("P", 6)])

        def proj_tm(b, j4, pbank, pkey):
            for jj in range(4):
                j = j4 * 4 + jj
                blk = self.blk_of_tile(j // 2)
                for kc in range(KC):
                    S.op("pe", lambda: nc.tensor.matmul(pbank[0:64, jj * 128:(jj + 1) * 128], self.HT[:, kc, j * 64:(j + 1) * 64], W[:, b, kc, :],
                                                        start=(kc == 0), stop=(kc == KC - 1)),
                         reads=[("W", b), ("HT", kc, blk)], writes=[pkey])

        def rr(gens):
            live = list(gens)
            while live:
                nxt = []
                for gq in live:
                    try:
                        next(gq)
                        nxt.append(gq)
                    except StopIteration:
                        pass
                live = nxt

        def p1_block(d, h, blk, st, wb):
            t0, n = BLKS[blk]
            ch0, nch = t0 // 64, n // 64
            B = BT[st]
            pz, pzk = (self.P[6], ("P", 6)) if st == 0 else (self.P[5], ("P", 5))
            sig, pre, ex, kkb = B["SIG"][:, 0:n], B["PRE"][:, 0:n], B["EX"][:, 0:n], B["KKb"][:, 0:n]
            ks, kp, ke, kk_ = ("SIG", st), ("PRE", st), ("EX", st), ("KKb", st)
            for kc in range(KC):
                S.op("pe", lambda: nc.tensor.matmul(pz[:, 0:n], W[:, wb, kc, :], self.HT[:, kc, t0:t0 + n], start=(kc == 0), stop=(kc == KC - 1)),
                     reads=[("W", wb), ("HT", kc, blk)], writes=[pzk])
            yield
            S.op("act", lambda: nc.scalar.activation(out=sig, in_=pz[:, 0:n], func=AF.Exp, scale=-1.0), reads=[pzk], writes=[ks])
            S.op("act", lambda: nc.scalar.activation(out=sig, in_=sig, func=AF.Ln, bias=ONEC), reads=[ks, "ONEC"], writes=[ks])
            S.op("act", lambda: nc.scalar.activation(out=sig, in_=sig, func=AF.Exp, scale=-1.0), reads=[ks], writes=[ks])
            yield
            S.op("dve", lambda: nc.vector.tensor_scalar(out=kkb, in0=sig, scalar1=NOML[:, d, h:h + 1], scalar2=OML[:, d, h:h + 1], op0=mul, op1=add),
                 reads=[ks, "NOML", "OML"], writes=[kk_])
            S.op("act", lambda: nc.scalar.activation(out=sig, in_=sig, func=AF.Ln, scale=OML[:, d, h:h + 1], bias=LBV[:, d, h:h + 1]),
                 reads=[ks, "OML", "LBV"], writes=[ks])
            yield
            S.op("dve", lambda: nc.vector.tensor_tensor_scan(out=pre, data0=CM512[:, 0:n], data1=sig, initial=0.0, op0=mul, op1=add),
                 reads=[ks, "CM"], writes=[kp])
            yield
            pre3 = pre.rearrange("p (j s) -> p j s", s=64)
            S.op("act", lambda: nc.scalar.activation(out=EBLd[d][:, ch0:ch0 + nch], in_=pre3[:, :, 63], func=AF.Exp), reads=[kp],
                 writes=[("EBL", d, blk)])
            if d == 0:
                bc, kb_, ex2, ke2 = pre, kp, sig, ks
            else:
                sig3 = sig.rearrange("p (j s) -> p j s", s=64)
                S.op("dve", lambda: nc.vector.tensor_tensor(out=sig, in0=sig, in1=pre, op=sub), reads=[ks, kp], writes=[ks])
                S.op("dve", lambda: nc.vector.tensor_tensor(out=sig3, in0=sig3, in1=pre3[:, :, 63:64].to_broadcast([128, nch, 64]), op=add),
                     reads=[ks, kp], writes=[ks])
                bc, kb_, ex2, ke2 = sig, ks, pre, kp
            S.op("act", lambda: nc.scalar.activation(out=bc, in_=bc, func=AF.Relu, bias=P80), reads=[kb_, ("EBL", d, blk), "C80"], writes=[kb_])
            yield
            S.op("act", lambda: nc.scalar.activation(out=ex, in_=bc, func=AF.Exp, bias=N80), reads=[kb_, "C80"], writes=[ke])
            S.op("act", lambda: nc.scalar.activation(out=ex2, in_=bc, func=AF.Exp, scale=-1.0, bias=P80), reads=[kb_, ("EBL", d, blk), "C80"], writes=[ke2])
            yield
            alias = ["OT"] if d == 1 else []
            S.op("dve", lambda: nc.vector.tensor_tensor(out=QTLd[d][:, t0:t0 + n], in0=QS[:, t0:t0 + n], in1=ex, op=mul),
                 reads=[("QS", blk), ke], writes=[("QTL", d, blk)] + alias)
            S.op("dve", lambda: nc.vector.tensor_tensor(out=KTLd[d][:, t0:t0 + n], in0=kkb, in1=ex2, op=mul),
                 reads=[kk_, ke2], writes=[("KTL", d, blk)] + alias)

        def phase1(d, h, wb):
            for a_ in range(0, 5, 2):
                live = [p1_block(d, h, blk, st, wb) for st, blk in enumerate(range(a_, min(a_ + 2, 5)))]
                while live:
                    nxt = []
                    for gq in live:
                        try:
                            next(gq)
                            nxt.append(gq)
                        except StopIteration:
                            pass
                    live = nxt
                    yield

        def phase2(d):
            QTL, KTL, EBL = QTLd[d], KTLd[d], EBLd[d]
            order = list(range(NCH)) if d == 0 else [3, 2, 1, 0] + list(range(NCH - 1, 3, -1))
            npos = {j: n_ for n_, j in enumerate(order)}
            eblk = [("EBL", d, blk) for blk in range(5)]
            cb = lambda j: self.blk_of_tile(j // 2)
            if d == 0:
                EBLN = EBL
            else:
                EBLN = EBLNB
                S.op("act", lambda: nc.scalar.activation(out=EBLNB[:, 0:4], in_=EBL[:, 3::-1], func=AF.Copy), reads=eblk, writes=["EBLN"])
                S.op("act", lambda: nc.scalar.activation(out=EBLNB[:, 4:NCH], in_=EBL[:, NCH - 1:3:-1], func=AF.Copy), reads=eblk, writes=["EBLN"])
            for ng in range(5):
                cnt = min(8, NCH - ng * 8)
                kb = ng % 2
                for jj in range(cnt):
                    j = order[ng * 8 + jj]
                    S.op("pe", lambda: nc.tensor.transpose(self.PTB[0:64, jj * 128:(jj + 1) * 128], KTL[:, j * 64:(j + 1) * 64], self.identb[:]),
                         reads=[("KTL", d, cb(j)), "identb"], writes=[("P", 7)])
                S.op("act", lambda: nc.scalar.activation(out=KTOKG[:, kb, 0:cnt, :], in_=self.PTB[0:64, 0:cnt * 128].rearrange("p (j k) -> p j k", k=128),
                                                          func=AF.Copy), reads=[("P", 7)], writes=[("KTOKG", kb)])
                yield
                for half in range((cnt + 3) // 4):
                    pb, pk = self.P[4], ("P", 4)
                    n0 = ng * 8 + half * 4
                    c4 = min(4, NCH - n0)
                    for jj in range(c4):
                        j = order[n0 + jj]
                        S.op("pe", lambda: nc.tensor.matmul(pb[:, jj * 128:(jj + 1) * 128], KTOKG[:, kb, half * 4 + jj, :], V[:, j, :], start=True, stop=True),
                             reads=[("KTOKG", kb), ("V", j // 4)], writes=[pk])
                    S.op("dve", lambda: nc.vector.tensor_tensor(out=EPN[:, n0:n0 + c4, :],
                                                                in0=pb[:, 0:c4 * 128].rearrange("p (n v) -> p n v", v=128),
                                                                in1=EBLN[:, n0:n0 + c4].unsqueeze(2).to_broadcast([128, c4, 128]), op=mul),
                         reads=[pk, "EBLN"] + eblk, writes=[("EP", n0 + q_) for q_ in range(c4)])
                    yield
            for n_ in range(NCH - 1):
                cur, prv = S32P[:, n_ % 2, :], S32P[:, (n_ + 1) % 2, :]
                if n_ == 0:
                    S.op("dve", lambda: nc.vector.tensor_copy(out=cur, in_=EPN[:, 0, :]), reads=[("EP", 0)], writes=[("S32", 0)])
                else:
                    S.op("dve", lambda: nc.vector.scalar_tensor_tensor(out=cur, in0=prv, scalar=EBLN[:, n_:n_ + 1], in1=EPN[:, n_, :],
                                                                       op0=mul, op1=add),
                         reads=[("S32", (n_ + 1) % 2), ("EP", n_), "EBLN"] + eblk, writes=[("S32", n_ % 2)])
                    S.op("act", lambda: nc.scalar.activation(out=EPN[:, n_, :], in_=cur, func=AF.Copy), reads=[("S32", n_ % 2)], writes=[("EP", n_)])
                if n_ % 3 == 2:
                    yield
            for jg in range(5):
                cnt = min(8, NCH - jg * 8)
                sb_ = jg % 2
                ps, psk = self.P[sb_], ("P", sb_)
                for jj in range(cnt):
                    j = jg * 8 + jj
                    c0 = j * 64
                    S.op("pe", lambda: nc.tensor.matmul(ps[0:64, jj * 64:(jj + 1) * 64], KTL[:, c0:c0 + 64], QTL[:, c0:c0 + 64], start=True, stop=True),
                         reads=[("KTL", d, cb(j)), ("QTL", d, cb(j))], writes=[psk])
                S.op("dve", lambda: nc.vector.tensor_tensor(out=SCG[:, sb_, 0:cnt, :], in0=ps[0:64, 0:cnt * 64].rearrange("p (j t) -> p j t", t=64),
                                                            in1=MSK[:, d, :].unsqueeze(1).to_broadcast([64, cnt, 64]), op=mul),
                     reads=[psk, "MSK"], writes=[("SCG", sb_)])
                yield
                for half in range((cnt + 3) // 4):
                    bi = 2 + (jg * 2 + half) % 2
                    po, pok = self.P[bi], ("P", bi)
                    j0 = jg * 8 + half * 4
                    c4 = min(4, NCH - j0)
                    for jj in range(c4):
                        j = j0 + jj
                        n_ = npos[j]
                        c0 = j * 64
                        S.op("pe", lambda: nc.tensor.matmul(po[0:64, jj * 128:(jj + 1) * 128], SCG[:, sb_, half * 4 + jj, :], V[:, j, :],
                                                            start=True, stop=(n_ == 0)),
                             reads=[("SCG", sb_), ("V", j // 4)], writes=[pok])
                        if n_ > 0:
                            S.op("pe", lambda: nc.tensor.matmul(po[0:64, jj * 128:(jj + 1) * 128], QTL[:, c0:c0 + 64], EPN[:, n_ - 1, :],
                                                                start=False, stop=True),
                                 reads=[("QTL", d, cb(j)), ("EP", n_ - 1)], writes=[pok])
                    src = po[0:64, 0:c4 * 128].rearrange("p (j v) -> p j v", v=128)
                    ok_ = [("OACC", j0 + jj) for jj in range(c4)]
                    if d == 0:
                        S.op("act", lambda: nc.scalar.activation(out=OACC[:, j0:j0 + c4, :], in_=src, func=AF.Copy), reads=[pok], writes=ok_)
                    else:
                        S.op("dve", lambda: nc.vector.tensor_tensor(out=OACC[:, j0:j0 + c4, :], in0=OACC[:, j0:j0 + c4, :], in1=src, op=add),
                             reads=[pok] + ok_, writes=ok_)
                    yield

        WOH2 = [WOH, self.RS[:].rearrange("p a n -> p (a n)").bitcast(BF16)]
        btkeys = [(nm, st_) for nm in ("SIG", "PRE", "EX", "KKb") for st_ in range(2)]
        oak = [("OACC", j) for j in range(NCH)]

        def qv_gen(h):
            S.dma("pool", WOH2[h % 2], w_out[h * 128:(h + 1) * 128, :], writes=[("WOH", h % 2)])
            S.op("act", lambda: nc.scalar.activation(out=WOH2[h % 2], in_=WOH2[h % 2], func=AF.Copy, scale=GCOL), reads=[("WOH", h % 2), "GCOL"],
                 writes=[("WOH", h % 2)])
            b = load_w(h * 128)
            for blk in range(5):
                t0, n = BLKS[blk]
                proj_fm(b, blk)
                S.op("act", lambda: nc.scalar.activation(out=QS[:, t0:t0 + n], in_=p6[:, 0:n], func=AF.Silu), reads=[("P", 6)], writes=[("QS", blk)])
                yield
            b = load_w(3072 + h * 128)
            for j4 in range(NCH // 4):
                pb, pk = self.P[4 + j4 % 2], ("P", 4 + j4 % 2)
                proj_tm(b, j4, pb, pk)
                src = pb[0:64, :].rearrange("p (j v) -> p j v", j=4)
                if j4 % 2 == 0:
                    S.op("dve", lambda: nc.vector.tensor_copy(out=V[:, j4 * 4:(j4 + 1) * 4, :], in_=src), reads=[pk], writes=[("V", j4)])
                else:
                    S.op("act", lambda: nc.scalar.activation(out=V[:, j4 * 4:(j4 + 1) * 4, :], in_=src, func=AF.Copy), reads=[pk], writes=[("V", j4)])
                yield

        def gproj_gen(h):
            b = load_w(4096 + h * 128)
            for j4 in range(NCH // 4):
                bi = 5 + j4 % 2
                pb, pk = self.P[bi], ("P", bi)
                proj_tm(b, j4, pb, pk)
                src = pb[0:64, :].rearrange("p (j v) -> p j v", j=4)
                S.op("act", lambda: nc.scalar.activation(out=GS[:, j4 * 4:(j4 + 1) * 4, :], in_=src, func=AF.Silu), reads=[pk],
                     writes=["GS"] + btkeys)
                yield

        def finish_gen(h):
            S.op("act", lambda: nc.scalar.activation(out=SQO, in_=OACC, func=AF.Square), reads=oak, writes=["SQO"])
            yield
            S.op("dve", lambda: nc.vector.reduce_sum(out=SSQ, in_=SQO, axis=AX.X), reads=["SQO"], writes=["SSQ"])
            S.op("act", lambda: nc.scalar.activation(out=RSTD, in_=SSQ, func=AF.Ln, scale=1.0 / 128, bias=self.epsT[0:64, 0:1]), reads=["SSQ", "epsT"],
                 writes=["RSTD"])
            S.op("act", lambda: nc.scalar.activation(out=RSTD, in_=RSTD, func=AF.Exp, scale=-0.5), reads=["RSTD"], writes=["RSTD"])
            yield
            S.op("dve", lambda: nc.vector.tensor_tensor(out=OACC, in0=OACC, in1=RSTD.unsqueeze(2).to_broadcast([64, NCH, 128]), op=mul),
                 reads=oak + ["RSTD"], writes=oak)
            yield
            S.op("dve", lambda: nc.vector.tensor_tensor(out=OB, in0=OACC, in1=GS, op=mul), reads=oak + ["GS"], writes=["GS"])
            yield
            done = 0
            while done < NCH:
                cnt = min(16, NCH - done)
                for jj in range(cnt):
                    S.op("pe", lambda: nc.tensor.transpose(self.PTB[:, jj * 64:(jj + 1) * 64], OB[:, done + jj, :], self.identb[0:64, 0:64]),
                         reads=["GS", "identb"], writes=[("P", 7)])
                S.op("dve", lambda: nc.vector.tensor_copy(out=OT[:, done * 64:(done + cnt) * 64], in_=self.PTB[:, 0:cnt * 64]),
                     reads=[("P", 7)], writes=["OT"])
                done += cnt
                yield

        def outproj_gen(h):
            wi = h % 2
            for blk in range(5):
                t0, n = BLKS[blk]
                col = 1 if blk == 0 else 0
                for oc in range(KC):
                    pp, ppk = self.P[oc % 2], ("P", oc % 2)
                    S.op("pe", lambda: nc.tensor.matmul(pp[:, 0:n], WOH2[wi][:, oc * 128:(oc + 1) * 128], OT[:, t0:t0 + n], start=True, stop=True),
                         reads=[("WOH", wi), "OT"], writes=[ppk])
                    xs = self.XT[:, oc, t0:t0 + n]
                    S.op("dve", lambda: nc.vector.scalar_tensor_tensor(out=xs, in0=pp[:, 0:n], scalar=self.MOD[:, 2, oc, col:col + 1], in1=xs,
                                                                       op0=mul, op1=add),
                         reads=[ppk, self.modkey + (2,), ("XT", oc, blk)], writes=[("XT", oc, blk)])
                    if oc % 2 == 1:
                        yield

        rr([qv_gen(0)])
        for h in range(8):
            S.barrier()
            wb0 = load_w(1024 + h * 128)
            rr([phase1(0, h, wb0)] + ([outproj_gen(h - 1)] if h > 0 else []))
            wb1 = load_w(2048 + h * 128)
            rr([phase1(1, h, wb1), phase2(0)])
            rr([phase2(1), gproj_gen(h)])
            S.barrier()
            rr([finish_gen(h)] + ([qv_gen(h + 1)] if h < 7 else []))
        S.barrier()
        rr([outproj_gen(7)])

    def emit_ffn(self, i):
        nc, S, I = self.nc, self.S, self.I
        blks = list(range(5)) if i < DEPTH - 1 else list(range(1, 5))
        WG = self.scr(0, 2 * KC * 256).rearrange("p (b k n) -> p b k n", b=2, k=KC)
        WU = self.scr(4096, 2 * KC * 256).rearrange("p (b k n) -> p b k n", b=2, k=KC)
        WO = self.scr(8192, 8 * 1024).rearrange("p (j n) -> p j n", j=8)
        SG = self.scr(16384, 2 * 512 * 2, F32).rearrange("p (b n) -> p b n", b=2)
        ACTT = self.big(0, 8 * T).rearrange("p (j t) -> p j t", j=8)
        self.adaln_finish(i, 2)
        self.norm_mod_all(blks, 2, lambda blk, c: (self.HT[:, c, BLKS[blk][0]:BLKS[blk][0] + BLKS[blk][1]], ("HT", c, blk)))
        wcount = 0
        sgi = 0
        pgi = 0
        ada_i = i + 1 if i + 1 < self.n_layers else None
        self.ada_next = 0
        for (j0, j1) in THIRDS:
            for jp in range(j0, j1, 2):
                b = wcount % 2
                wcount += 1
                S.dma_split("pool", WG[:, b], I["ffn_w_in"][i, :, jp * 128:jp * 128 + 256].rearrange("(k p) n -> p k n", p=128), ("WG", b))
                S.dma_split("pool", WU[:, b], I["ffn_w_in"][i, :, DFF + jp * 128:DFF + jp * 128 + 256].rearrange("(k p) n -> p k n", p=128), ("WU", b))
                for jj in range(2):
                    j = jp + jj
                    for blk in blks:
                        t0, n = BLKS[blk]
                        if ada_i is not None and blk in (1, 3):
                            if self.ada_next < 24:
                                self.adaln_block(ada_i, self.ada_next)
                                self.ada_next += 1
                            else:
                                self.adaln_flush()
                        pg = self.P[pgi % 2]
                        pu = self.P[2 + pgi % 2]
                        kg, ku = ("P", pgi % 2), ("P", 2 + pgi % 2)
                        pgi += 1
                        hk = [("HT", c, blk) for c in range(KC)]
                        for kc in range(KC):
                            S.op("pe", lambda: nc.tensor.matmul(pg[:, 0:n], WG[:, b, kc, jj * 128:(jj + 1) * 128], self.HT[:, kc, t0:t0 + n],
                                                                start=(kc == 0), stop=(kc == KC - 1)),
                                 reads=[("WG", b), hk[kc]], writes=[kg])
                        for kc in range(KC):
                            S.op("pe", lambda: nc.tensor.matmul(pu[:, 0:n], WU[:, b, kc, jj * 128:(jj + 1) * 128], self.HT[:, kc, t0:t0 + n],
                                                                start=(kc == 0), stop=(kc == KC - 1)),
                                 reads=[("WU", b), hk[kc]], writes=[ku])
                        sg = SG[:, sgi % 2, 0:n]
                        sk = ("SG", sgi % 2)
                        sgi += 1
                        S.op("act", lambda: nc.scalar.activation(out=sg, in_=pg[:, 0:n], func=AF.Silu), reads=[kg], writes=[sk])
                        S.op("dve", lambda: nc.vector.tensor_tensor(out=ACTT[:, j - j0, t0:t0 + n], in0=sg, in1=pu[:, 0:n], op=ALU.mult),
                             reads=[sk, ku], writes=[("ACTT", j - j0, blk)])
            nj = j1 - j0
            S.dma_split("pool", WO[:, 0:nj, :], I["ffn_w_out"][i, j0 * 128:j1 * 128, :].rearrange("(j p) n -> p j n", p=128), "WO")
            ngen = None
            if (j0, j1) == THIRDS[-1] and ada_i is not None and (ada_i % 4) != 0:
                self.adaln_finish(ada_i, 1)
                ngen = self.norm_mod_gen(list(range(5)), 1, lambda blk, c: (self.HT[:, c, BLKS[blk][0]:BLKS[blk][0] + BLKS[blk][1]], ("HT", c, blk)),
                                         MOD=self.MODS[ada_i % 2], modkey=("MOD", ada_i % 2))
                self.norm_done = True
            for blk in blks:
                if ngen is not None and blk > blks[0]:
                    next(ngen, None)
                t0, n = BLKS[blk]
                col = 1 if blk == 0 else 0
                for oc in range(KC):
                    po = self.P[4 + oc % 2]
                    pk = ("P", 4 + oc % 2)
                    for jj in range(nj):
                        S.op("pe", lambda: nc.tensor.matmul(po[:, 0:n], WO[:, jj, oc * 128:(oc + 1) * 128], ACTT[:, jj, t0:t0 + n],
                                                            start=(jj == 0), stop=(jj == nj - 1)),
                             reads=["WO", ("ACTT", jj, blk)], writes=[pk])
                    xs = self.XT[:, oc, t0:t0 + n]
                    S.op("dve", lambda: nc.vector.scalar_tensor_tensor(out=xs, in0=po[:, 0:n], scalar=self.MOD[:, 5, oc, col:col + 1], in1=xs,
                                                                       op0=ALU.mult, op1=ALU.add),
                         reads=[pk, self.modkey + (5,), ("XT", oc, blk)], writes=[("XT", oc, blk)])
                if (j0, j1) == THIRDS[-1] and i == self.n_layers - 1 and not self.debug and blk >= 1:
                    XO = self.scr(18432, 4096, F32).rearrange("p (a n) -> p a n", a=2)
                    obanks = [[(self.P[6], ("P", 6))], [(self.PTB[:].bitcast(F32), ("P", 7))]]
                    for tg in range(2 + (blk - 1) * 4, 2 + blk * 4):
                        self.output_tile(tg, XO, obanks)
                        self.out_done.add(tg)
            if ngen is not None:
                for _ in ngen:
                    pass

    def output_tile(self, tg, XIO, banks):
        nc, S = self.nc, self.S
        blk = self.blk_of_tile(tg)
        si = tg % 2
        for h in range(2):
            pb, pk = banks[h][tg % len(banks[h])]
            for cc in range(4):
                c = h * 4 + cc
                S.op("pe", lambda: nc.tensor.transpose(pb[:, cc * 128:(cc + 1) * 128], self.XT[:, c, tg * 128:(tg + 1) * 128], self.identf[:]),
                     reads=[("XT", c, blk), "identf"], writes=[pk])
            dst = XIO[:, si, h * 512:(h + 1) * 512]
            if h == 0:
                S.op("dve", lambda: nc.vector.tensor_copy(out=dst, in_=pb[:, 0:512]), reads=[pk], writes=[("STG", si, h)])
            else:
                S.op("act", lambda: nc.scalar.activation(out=dst, in_=pb[:, 0:512], func=AF.Copy), reads=[pk], writes=[("STG", si, h)])
        if self.debug:
            key = ("dbg", tg)
            S.dma("sp", self.dbg[tg * 128:(tg + 1) * 128, :], XIO[:, si, :], reads=[("STG", si, 0), ("STG", si, 1)], writes=[key])
            self.outkeys.append(key)
        if tg >= 2:
            key = ("out", tg)
            S.dma("sp", self.out[(tg - 2) * 128:(tg - 1) * 128, :], XIO[:, si, :], reads=[("STG", si, 0), ("STG", si, 1)], writes=[key])
            self.outkeys.append(key)

    def emit_output(self):
        S = self.S
        XIO = self.scr(0, 4096, F32).rearrange("p (a n) -> p a n", a=2)
        banks = [[(self.P[0], ("P", 0)), (self.P[2], ("P", 2))], [(self.P[1], ("P", 1)), (self.P[3], ("P", 3))]]
        tiles = range(NT) if self.debug else range(2, NT)
        for tg in tiles:
            if tg not in self.out_done:
                self.output_tile(tg, XIO, banks)
        S.wait_all("sp", self.outkeys)


def make_in_maps(inputs):
    cst = make_consts()
    f = lambda a: np.ascontiguousarray(np.asarray(a, dtype=np.float32))
    shared = {
        "c_ctx": f(inputs["c_ctx"]).reshape(KC, 128),
        "w_ada": f(inputs["w_ada"]),
        "b_ada": f(inputs["b_ada"]).reshape(DEPTH * 48, 128),
        "norm_gain": f(inputs["norm_gain"]).reshape(DEPTH * 2 * KC, 128),
        "ffn_w_in": f(inputs["ffn_w_in"]),
        "ffn_w_out": f(inputs["ffn_w_out"]),
        "fnet_w_out": f(inputs["fnet_w_out"]).reshape(D, D),
        "fnet_b_out": f(inputs["fnet_b_out"]).reshape(KC, 128),
        "diff_w_in": f(inputs["diff_w_in"]).reshape(D, 3 * D),
        "diff_w_out": f(inputs["diff_w_out"]).reshape(D, D),
        "diff_lambda": f(inputs["diff_lambda"]).reshape(256),
        "diff_subln": f(inputs["diff_subln_gain"]).reshape(128),
        "hgrn_w_in": f(inputs["hgrn_w_in"]).reshape(D, 5 * D),
        "hgrn_w_out": f(inputs["hgrn_w_out"]).reshape(D, D),
        "hgrn_lb": f(inputs["hgrn_lower_bound"]).reshape(64, 128),
        "hgrn_gain": f(inputs["hgrn_norm_gain"]).reshape(128),
        "gqa_w_in": f(inputs["gqa_w_in"]).reshape(D, 1536),
        "gqa_w_out": f(inputs["gqa_w_out"]).reshape(D, D),
    }
    p = np.arange(128)
    r64 = (p - p % 64) + (p % 64 + 32) % 64
    r128 = (p + 64) % 128
    dq = f(inputs["diff_q_gain"]).reshape(128)
    dk = f(inputs["diff_k_gain"]).reshape(128)
    shared["diff_g"] = np.ascontiguousarray(np.stack([dq, dq[r64], dk, dk[r64]], axis=0))
    gq = f(inputs["gqa_q_gain"]).reshape(128)
    gk = f(inputs["gqa_k_gain"]).reshape(128)
    shared["gqa_g"] = np.ascontiguousarray(np.stack([gq, gq[r128], gk, gk[r128]], axis=0))
    shared.update(cst)
    maps = []
    for b in range(8):
        m = dict(shared)
        m["x"] = f(inputs["x"][b])
        m["c"] = f(inputs["c"][b]).reshape(KC, 128)
        m["ctx"] = f(inputs["ctx"][b])
        maps.append(m)
    return maps


_NC_CACHE = {}


def kernel(**inputs):
    if "nc" not in _NC_CACHE:
        _NC_CACHE["nc"] = Builder().build()
    nc = _NC_CACHE["nc"]
    in_maps = make_in_maps(inputs)
    res = run_bass_kernel_spmd(nc, in_maps, core_ids=list(range(8)))
    return np.stack([np.asarray(r["out"], dtype=np.float32) for r in res.results], axis=0)
```
